# Optimizing a Trainium2 kernel written in Bass

```python
import math
import jax, jax.numpy as jnp
from jax import lax
import numpy as np

D_MODEL = 1024
BATCH = 16
SEQ = 256
DEPTH = 4
DEC_BATCH = 8
DEC_SEQ = 2048
PAST_LEN = 256

GRID_W = 64
N_MIXERS = 2
N_RG = (DEPTH + 1) // 2
N_MLA = DEPTH // 2
D_FF = 4 * D_MODEL
D_RNN = D_MODEL
RG_BLOCKS = 8
RG_BS = D_RNN // RG_BLOCKS
RG_C = 8.0
CONV_W = 4
CONV_LEFT = (CONV_W - 1) // 2
N_HEADS = 8
QK_NOPE = 128
QK_ROPE = 64
V_DIM = 128
Q_LORA = 512
KV_LORA = 256
ROPE_THETA = 10000.0
Q_BLOCK = 128
EPS = 1e-6

kernel_name = "hybrid_rglru_mla_diffusion_step"


def rms_norm(x, g):
    xf = x.astype(jnp.float32)
    y = xf * lax.rsqrt(jnp.mean(xf * xf, axis=-1, keepdims=True) + EPS)
    return (y * g.astype(jnp.float32)).astype(x.dtype)


def adaln_params(cond, w_ada, b_ada):
    mod = jax.nn.silu(cond) @ w_ada + b_ada
    return jnp.split(mod, 6, axis=-1)


def modulate(h, shift, scale):
    return h * (1 + scale[:, None, :]) + shift[:, None, :]


def sq_relu_mlp(h, w1, w2):
    a = jax.nn.relu(h @ w1)
    return (a * a) @ w2


def depthwise_conv(x, w, b):
    T = x.shape[1]
    xp = jnp.pad(x, ((0, 0), (CONV_LEFT, CONV_W - 1 - CONV_LEFT), (0, 0)))
    out = xp[:, 0:T] * w[0]
    for k in range(1, CONV_W):
        out = out + xp[:, k:k + T] * w[k]
    return out + b


def linear_scan(a, b, h0, reverse):
    idx = -1 if reverse else 0
    b = b.at[:, idx].add(a[:, idx] * h0)

    def combine(e1, e2):
        a1, b1 = e1
        a2, b2 = e2
        return a1 * a2, a2 * b1 + b2

    _, h = lax.associative_scan(combine, (a, b), axis=1, reverse=reverse)
    return h


def rglru_block(h, h0, w_in, conv_w, conv_b, wa, ba, wx, bx, lam, w_out):
    B, T, _ = h.shape
    xb, yb = jnp.split(h @ w_in, 2, axis=-1)
    yb = jax.nn.gelu(yb)
    xb = depthwise_conv(xb, conv_w, conv_b)
    xf = xb.astype(jnp.float32)
    xblk = xf.reshape(B, T, RG_BLOCKS, RG_BS)
    r = jnp.einsum('btnk,rnkj->rbtnj', xblk, wa.astype(jnp.float32)).reshape(2, B, T, D_RNN)
    i = jnp.einsum('btnk,rnkj->rbtnj', xblk, wx.astype(jnp.float32)).reshape(2, B, T, D_RNN)
    r = jax.nn.sigmoid(r + ba.astype(jnp.float32)[:, None, None, :])
    i = jax.nn.sigmoid(i + bx.astype(jnp.float32)[:, None, None, :])
    log_a = -RG_C * r * jax.nn.softplus(-lam.astype(jnp.float32))[:, None, None, :]
    a = jnp.exp(log_a)
    b = jnp.sqrt(-jnp.expm1(2.0 * log_a)) * (i * xf[None])
    h0f = h0.astype(jnp.float32)
    h_f = linear_scan(a[0], b[0], h0f[:, 0], reverse=False)
    h_b = linear_scan(a[1], b[1], h0f[:, 1], reverse=True)
    final = jnp.stack([h_f[:, -1], h_b[:, 0]], axis=1).astype(h.dtype)
    out = ((h_f + h_b).astype(h.dtype) * yb) @ w_out
    return out, final


def axial_rope_tables(n):
    rows = n // GRID_W
    row = jnp.repeat(jnp.arange(rows, dtype=jnp.float32), GRID_W)
    col = jnp.tile(jnp.arange(GRID_W, dtype=jnp.float32), rows)
    half = QK_ROPE // 2
    inv = ROPE_THETA ** (-jnp.arange(0, half, 2, dtype=jnp.float32) / half)
    ang = jnp.concatenate([row[:, None] * inv, col[:, None] * inv], axis=-1)
    return jnp.cos(ang), jnp.sin(ang)


def apply_rope(x, cos, sin):
    xf = x.astype(jnp.float32).reshape(x.shape[:-1] + (QK_ROPE // 2, 2))
    x1, x2 = xf[..., 0], xf[..., 1]
    out = jnp.stack([x1 * cos - x2 * sin, x1 * sin + x2 * cos], axis=-1)
    return out.reshape(x.shape).astype(x.dtype)


def mla_project(h, w_dqkv, g_q, g_kv, w_uq):
    B, T, _ = h.shape
    proj = h @ w_dqkv
    cq = rms_norm(proj[..., :Q_LORA], g_q)
    ckv = rms_norm(proj[..., Q_LORA:Q_LORA + KV_LORA], g_kv)
    kpe = proj[..., Q_LORA + KV_LORA:]
    q = (cq @ w_uq).reshape(B, T, N_HEADS, QK_NOPE + QK_ROPE)
    return q[..., :QK_NOPE], q[..., QK_NOPE:], ckv, kpe


def mla_kv(ckv, w_ukv):
    B, T, _ = ckv.shape
    kv = (ckv @ w_ukv).reshape(B, T, N_HEADS, QK_NOPE + V_DIM)
    return kv[..., :QK_NOPE], kv[..., QK_NOPE:]


def block_attention(q_nope, q_pe, k_nope, k_pe, v):
    B, S = q_nope.shape[:2]
    nb = S // Q_BLOCK
    qn = q_nope.reshape(B, nb, Q_BLOCK, N_HEADS, QK_NOPE).transpose(1, 0, 2, 3, 4)
    qp = q_pe.reshape(B, nb, Q_BLOCK, N_HEADS, QK_ROPE).transpose(1, 0, 2, 3, 4)
    scale = (QK_NOPE + QK_ROPE) ** -0.5

    def one_block(args):
        qn_b, qp_b = args
        s = (jnp.einsum('bqhd,bkhd->bhqk', qn_b, k_nope)
             + jnp.einsum('bqhd,bkd->bhqk', qp_b, k_pe))
        p = jax.nn.softmax(s.astype(jnp.float32) * scale, axis=-1).astype(v.dtype)
        return jnp.einsum('bhqk,bkhd->bqhd', p, v)

    o = lax.map(one_block, (qn, qp))
    return o.transpose(1, 0, 2, 3, 4).reshape(B, S, N_HEADS * V_DIM)


def setup_inputs(seed: int = 0) -> dict:
    key = jax.random.key(seed)
    ks = iter(jax.random.split(key, 40))
    f32 = jnp.float32

    def nrm(shape, scale):
        return jax.random.normal(next(ks), shape, f32) * scale

    u = jax.random.uniform(next(ks), (N_RG, 2, D_RNN), f32, 0.9, 0.999)
    rg_lambda = jnp.log(u) - jnp.log1p(-u)
    return {
        "x_prompt": nrm((BATCH, SEQ, D_MODEL), 1.0),
        "x_sample": nrm((DEC_BATCH, DEC_SEQ, D_MODEL), 1.0),
        "state_rglru": nrm((DEC_BATCH, N_RG, 2, D_RNN), 0.5),
        "cache_ckv": nrm((DEC_BATCH, N_MLA, PAST_LEN, KV_LORA), 1.0),
        "cache_kpe": nrm((DEC_BATCH, N_MLA, PAST_LEN, QK_ROPE), 1.0),
        "c": nrm((DEC_BATCH, D_MODEL), 1.0),
        "c_ctx": nrm((D_MODEL,), 1.0),
        "ada_w": nrm((DEPTH, D_MODEL, 6 * D_MODEL), 0.5 * D_MODEL ** -0.5),
        "ada_b": nrm((DEPTH, 6 * D_MODEL), 0.02),
        "norm_mix": 1.0 + nrm((DEPTH, D_MODEL), 0.02),
        "norm_mlp": 1.0 + nrm((DEPTH, D_MODEL), 0.02),
        "mlp_w1": nrm((DEPTH, D_MODEL, D_FF), D_MODEL ** -0.5),
        "mlp_w2": nrm((DEPTH, D_FF, D_MODEL), D_FF ** -0.5),
        "rg_w_in": nrm((N_RG, D_MODEL, 2 * D_RNN), D_MODEL ** -0.5),
        "rg_conv_w": nrm((N_RG, CONV_W, D_RNN), CONV_W ** -0.5),
        "rg_conv_b": nrm((N_RG, D_RNN), 0.01),
        "rg_wa": nrm((N_RG, 2, RG_BLOCKS, RG_BS, RG_BS), RG_BS ** -0.5),
        "rg_ba": nrm((N_RG, 2, D_RNN), 0.01),
        "rg_wx": nrm((N_RG, 2, RG_BLOCKS, RG_BS, RG_BS), RG_BS ** -0.5),
        "rg_bx": nrm((N_RG, 2, D_RNN), 0.01),
        "rg_lambda": rg_lambda,
        "rg_w_out": nrm((N_RG, D_RNN, D_MODEL), D_RNN ** -0.5),
        "mla_w_dqkv": nrm((N_MLA, D_MODEL, Q_LORA + KV_LORA + QK_ROPE), D_MODEL ** -0.5),
        "mla_norm_q": 1.0 + nrm((N_MLA, Q_LORA), 0.02),
        "mla_norm_kv": 1.0 + nrm((N_MLA, KV_LORA), 0.02),
        "mla_w_uq": nrm((N_MLA, Q_LORA, N_HEADS * (QK_NOPE + QK_ROPE)), Q_LORA ** -0.5),
        "mla_w_ukv": nrm((N_MLA, KV_LORA, N_HEADS * (QK_NOPE + V_DIM)), KV_LORA ** -0.5),
        "mla_w_o": nrm((N_MLA, N_HEADS * V_DIM, D_MODEL), (N_HEADS * V_DIM) ** -0.5),
        "final_norm": 1.0 + nrm((D_MODEL,), 0.02),
    }


def reference(x_prompt, x_sample, state_rglru, cache_ckv, cache_kpe, c, c_ctx,
              ada_w, ada_b, norm_mix, norm_mlp, mlp_w1, mlp_w2,
              rg_w_in, rg_conv_w, rg_conv_b, rg_wa, rg_ba, rg_wx, rg_bx, rg_lambda, rg_w_out,
              mla_w_dqkv, mla_norm_q, mla_norm_kv, mla_w_uq, mla_w_ukv, mla_w_o,
              final_norm):
    xc = x_prompt
    cond_ctx = jnp.broadcast_to(c_ctx, (xc.shape[0], D_MODEL))
    rg_states, ckv_list, kpe_list = [], [], []
    for l in range(DEPTH):
        sh1, sc1, g1, sh2, sc2, g2 = adaln_params(cond_ctx, ada_w[l], ada_b[l])
        h = modulate(rms_norm(xc, norm_mix[l]), sh1, sc1)
        j = l // N_MIXERS
        if l % N_MIXERS == 0:
            h0 = jnp.zeros((xc.shape[0], 2, D_RNN), xc.dtype)
            out, fin = rglru_block(h, h0, rg_w_in[j], rg_conv_w[j], rg_conv_b[j], rg_wa[j], rg_ba[j],
                                   rg_wx[j], rg_bx[j], rg_lambda[j], rg_w_out[j])
            rg_states.append(fin)
        else:
            q_nope, q_pe, ckv, kpe = mla_project(h, mla_w_dqkv[j], mla_norm_q[j], mla_norm_kv[j], mla_w_uq[j])
            k_nope, v = mla_kv(ckv, mla_w_ukv[j])
            out = block_attention(q_nope, q_pe, k_nope, kpe, v) @ mla_w_o[j]
            ckv_list.append(ckv)
            kpe_list.append(kpe)
        xc = xc + g1[:, None, :] * out
        h = modulate(rms_norm(xc, norm_mlp[l]), sh2, sc2)
        xc = xc + g2[:, None, :] * sq_relu_mlp(h, mlp_w1[l], mlp_w2[l])
    y_prompt = rms_norm(xc, final_norm)
    new_state_rglru = jnp.stack(rg_states, axis=1)
    new_cache_ckv = jnp.stack(ckv_list, axis=1)
    new_cache_kpe = jnp.stack(kpe_list, axis=1)

    xs = x_sample
    n_lat = xs.shape[1]
    cos, sin = axial_rope_tables(n_lat)
    for l in range(DEPTH):
        sh1, sc1, g1, sh2, sc2, g2 = adaln_params(c, ada_w[l], ada_b[l])
        h = modulate(rms_norm(xs, norm_mix[l]), sh1, sc1)
        j = l // N_MIXERS
        if l % N_MIXERS == 0:
            out, _ = rglru_block(h, state_rglru[:, j], rg_w_in[j], rg_conv_w[j], rg_conv_b[j], rg_wa[j],
                                 rg_ba[j], rg_wx[j], rg_bx[j], rg_lambda[j], rg_w_out[j])
        else:
            q_nope, q_pe, ckv, kpe = mla_project(h, mla_w_dqkv[j], mla_norm_q[j], mla_norm_kv[j], mla_w_uq[j])
            q_pe = apply_rope(q_pe, cos[:, None, :], sin[:, None, :])
            kpe = apply_rope(kpe, cos, sin)
            k_nope_l, v_l = mla_kv(ckv, mla_w_ukv[j])
            k_nope_c, v_c = mla_kv(cache_ckv[:, j], mla_w_ukv[j])
            k_nope = jnp.concatenate([k_nope_l, k_nope_c], axis=1)
            k_pe = jnp.concatenate([kpe, cache_kpe[:, j]], axis=1)
            v = jnp.concatenate([v_l, v_c], axis=1)
            out = block_attention(q_nope, q_pe, k_nope, k_pe, v) @ mla_w_o[j]
        xs = xs + g1[:, None, :] * out
        h = modulate(rms_norm(xs, norm_mlp[l]), sh2, sc2)
        xs = xs + g2[:, None, :] * sq_relu_mlp(h, mlp_w1[l], mlp_w2[l])
    y_sample = rms_norm(xs, final_norm)

    return (y_prompt, y_sample, new_state_rglru, new_cache_ckv, new_cache_kpe)
```

```python
import contextlib
import numpy as np
import concourse.bass as bass
import concourse.mybir as mybir
from concourse.bass_utils import run_bass_kernel_spmd

F32 = mybir.dt.float32
BF16 = mybir.dt.bfloat16
AF = mybir.ActivationFunctionType
ALU = mybir.AluOpType

NCORE = 8
T = 2560
NG = 5
GS = 512
EPS = 1e-6
SAME_ENG_SYNC = True


class Prog:
    def __init__(self, nc):
        self.nc = nc
        self.engs = ['pe', 'act', 'dve', 'pool', 'sp']
        self.ops = {e: [] for e in self.engs}
        self.st = {}
        self.dma_cnt = {}
        self.pend = {e: [] for e in self.engs}

    def op(self, eng, fn, r=(), w=(), dma=None):
        deps = self.pend[eng]
        self.pend[eng] = []
        for k in r:
            s = self.st.get(k)
            if s is not None and s[0] is not None:
                deps.append(s[0])
        for k in w:
            s = self.st.get(k)
            if s is not None:
                if s[0] is not None:
                    deps.append(s[0])
                deps.extend(s[1])
        rec = {'eng': eng, 'fn': fn, 'deps': deps, 'sig': False, 'dma': None, 'tick': 0}
        if dma is not None:
            c = self.dma_cnt.get(dma, 0) + 1
            self.dma_cnt[dma] = c
            rec['dma'] = (dma, 16 * c)
        self.ops[eng].append(rec)
        for k in r:
            s = self.st.get(k)
            if s is None:
                self.st[k] = [None, [rec]]
            else:
                s[1].append(rec)
        for k in w:
            self.st[k] = [rec, []]
        return rec

    def barrier(self):
        lasts = []
        for e in ('pe', 'act', 'dve', 'sp'):
            if self.ops[e]:
                lasts.append(self.ops[e][-1])
        for e in ('pe', 'act', 'dve', 'sp'):
            self.pend[e] = list(self.pend[e]) + lasts + list(self._last_dma.values())

    _last_dma = {}

    def _need(self, d, rec):
        if d is rec:
            return False
        if d['dma'] is not None:
            return True
        if d['eng'] == rec['eng']:
            if d['eng'] == 'pe':
                return False
            return SAME_ENG_SYNC
        return True

    def emit(self, final_sems):
        nc = self.nc
        for e in self.engs:
            for i, rec in enumerate(self.ops[e]):
                rec['idx'] = i
        for e in self.engs:
            for rec in self.ops[e]:
                best = {}
                dm = {}
                for d in rec['deps']:
                    if not self._need(d, rec):
                        continue
                    if d['dma'] is not None:
                        k = d['dma'][0]
                        if k not in dm or dm[k]['dma'][1] < d['dma'][1]:
                            dm[k] = d
                    else:
                        k = d['eng']
                        if k not in best or best[k]['idx'] < d['idx']:
                            best[k] = d
                for d in best.values():
                    d['sig'] = True
                rec['deps'] = list(best.values()) + list(dm.values())
        for e in self.engs:
            c = 0
            for rec in self.ops[e]:
                if rec['sig'] and rec['dma'] is None:
                    c += 1
                    rec['tick'] = c
        import os
        if os.environ.get('KDEBUG'):
            print('ops', {e: len(self.ops[e]) for e in self.engs})
            print('ticks', {e: max([r['tick'] for r in self.ops[e]] + [0]) for e in self.engs})
            print('dma', {k: 16 * v for k, v in self.dma_cnt.items()})
        with contextlib.ExitStack() as es:
            sems = {e: es.enter_context(nc.semaphore('s_' + e)) for e in self.engs}
            dsems = {k: es.enter_context(nc.semaphore('d_' + str(k))) for k in self.dma_cnt}
            block = es.enter_context(nc.Block())

            def run(e, eo):
                waited = {}
                for rec in self.ops[e]:
                    for d in rec['deps']:
                        if not self._need(d, rec):
                            continue
                        if d['dma'] is not None:
                            key = ('d', d['dma'][0]); val = d['dma'][1]; sem = dsems[d['dma'][0]]
                        else:
                            key = ('e', d['eng']); val = d['tick']; sem = sems[d['eng']]
                        if waited.get(key, 0) < val:
                            eo.wait_ge(sem, val)
                            waited[key] = val
                    ins = rec['fn'](eo)
                    if rec['dma'] is not None:
                        ins.then_inc(dsems[rec['dma'][0]], 16)
                    elif rec['sig']:
                        ins.then_inc(sems[e], 1)
                if e == 'sp':
                    for k in final_sems:
                        if k in self.dma_cnt:
                            eo.wait_ge(dsems[k], 16 * self.dma_cnt[k])

            block.tensor(lambda eo: run('pe', eo))
            block.scalar(lambda eo: run('act', eo))
            block.vector(lambda eo: run('dve', eo))
            block.gpsimd(lambda eo: run('pool', eo))
            block.sync(lambda eo: run('sp', eo))


def fm(v):
    v = np.asarray(v, np.float32)
    F = v.shape[-1]
    a = v.reshape(-1, F // 128, 128)
    return np.ascontiguousarray(a.transpose(2, 0, 1).reshape(128, -1))


VEC_SPEC = [
    ('ada_b', 192), ('norm_mix', 32), ('norm_mlp', 32), ('final', 8), ('conv_w', 64), ('conv_b', 16),
    ('ba', 32), ('bx', 32), ('lam', 32), ('normq', 8), ('normkv', 4), ('state', 32), ('cond', 16),
]
VOFF = {}
_o = 0
for _n, _c in VEC_SPEC:
    VOFF[_n] = _o
    _o += _c
NV = _o


def build_program(debug=False):
    nc = bass.Bass("TRN2", target_bir_lowering=False)
    P = Prog(nc)

    def din(name, shape):
        return nc.dram_tensor(name, list(shape), F32, kind="ExternalInput").ap()

    def dout(name, shape):
        return nc.dram_tensor(name, list(shape), F32, kind="ExternalOutput").ap()

    d_xs = din('xs', (2048, 1024)); d_xp = din('xp', (512, 1024))
    d_cckv = din('cckv', (2, 256, 256)); d_ckpe = din('ckpe', (2, 256, 64))
    d_vecs = din('vecs', (128, NV)); d_cst = din('cst', (128, 192))
    d_ropec = din('ropec', (64, 2048)); d_ropes = din('ropes', (64, 2048))
    d_ada_w = din('ada_w', (4, 1024, 6144))
    d_w1 = din('mlp_w1', (4, 1024, 4096)); d_w2 = din('mlp_w2', (4, 4096, 1024))
    d_win = din('rg_w_in', (2, 1024, 2048)); d_wa = din('rg_wa', (2, 2, 8, 128, 128)); d_wx = din('rg_wx', (2, 2, 8, 128, 128))
    d_wout = din('rg_w_out', (2, 1024, 1024))
    d_dqkv = din('mla_w_dqkv', (2, 1024, 832)); d_uq = din('mla_w_uq', (2, 512, 1536))
    d_ukv = din('mla_w_ukv', (2, 256, 2048)); d_wo = din('mla_w_o', (2, 1024, 1024))
    o_yp = dout('yp', (512, 1024)); o_ys = dout('ys', (2048, 1024))
    o_st = dout('nstate', (64, 128)); o_ckv = dout('nckv', (2, 2, 256, 256)); o_kpe = dout('nkpe', (2, 2, 256, 64))

    ARENA = 212800
    arena = nc.alloc_sbuf_tensor('arena', [128, ARENA // 4], F32)
    base = nc.lookup_mloc(arena).addr
    cur = [0]

    def alloc(name, shape, dt):
        nbytes = int(np.prod(shape[1:])) * (4 if dt == F32 else 2)
        nbytes = (nbytes + 31) // 32 * 32
        t = nc.alloc_sbuf_tensor_at(name, list(shape), dt, offset=base + cur[0])
        cur[0] += nbytes
        assert cur[0] <= ARENA, (name, cur[0])
        return t

    xT = alloc('xT', [128, 8, T], F32)
    hT = alloc('hT', [128, 8, T], BF16)
    NS = 5
    slots = [alloc('slot%d' % i, [128, 4096], BF16) for i in range(NS)]
    vecs = alloc('vecs', [128, NV], F32)
    cst = alloc('cst', [128, 192], F32)
    mod = alloc('mod', [128, 4, 48, 2], F32)
    AA = alloc('AA', [128, 4, 2, 2, 8], F32)
    rgd = alloc('rgd', [128, 4, 32], F32)
    ones_m = alloc('ones_m', [128, 3, 128], BF16)
    ones_1 = alloc('ones_1', [128, 128], BF16)
    cT = alloc('cT', [128, 2, 8], BF16)
    stst = alloc('stst', [128, 64], F32)
    epst = alloc('epst', [128, 1], F32)
    scr_base = cur[0]
    SCR = ARENA - scr_base

    scr_names = [0]

    def salloc_reset():
        cur[0] = scr_base

    def salloc(shape, dt):
        scr_names[0] += 1
        return alloc('scr%d' % scr_names[0], shape, dt)

    ident = cst[:, 0:128]
    Rm = cst[0:64, 128:192]

    psb = [nc.alloc_psum_tensor('ps%d' % i, [128, 512], F32) for i in range(8)]
    ps_pool = [list(range(8))]
    ps_i = [0]

    def psnext():
        b = ps_pool[0][ps_i[0] % len(ps_pool[0])]
        ps_i[0] += 1
        return b

    def mm(out, lhsT, rhs, start, stop, r, w):
        P.op('pe', lambda e: e.matmul(out, lhsT, rhs, start=start, stop=stop), r, w)

    def tr(out, in_, idn, r, w):
        P.op('pe', lambda e: e.transpose(out, in_, idn), r, w)

    def act(out, in_, func, r, w, bias=None, scale=None):
        kw = {}
        if bias is not None:
            kw['bias'] = bias
        if scale is not None:
            kw['scale'] = scale
        P.op('act', lambda e: e.activation(out, in_, func, **kw), r, w)

    def ts(out, in0, s1, s2, op0, op1, r, w):
        if s2 is None:
            P.op('dve', lambda e: e.tensor_scalar(out, in0, s1, None, op0), r, w)
        else:
            P.op('dve', lambda e: e.tensor_scalar(out, in0, s1, s2, op0, op1), r, w)

    def stt(out, in0, sc, in1, op0, op1, r, w):
        P.op('dve', lambda e: e.scalar_tensor_tensor(out, in0, sc, in1, op0, op1), r, w)

    def tt(out, in0, in1, op, r, w):
        P.op('dve', lambda e: e.tensor_tensor(out, in0, in1, op), r, w)

    def dcopy(out, in_, r, w):
        P.op('dve', lambda e: e.tensor_scalar(out, in_, 1.0, None, ALU.mult), r, w)

    def recip(out, in_, r, w):
        P.op('dve', lambda e: e.reciprocal(out, in_), r, w)

    def scan(out, a, b, init, r, w):
        P.op('dve', lambda e: e.tensor_tensor_scan(out, a, b, init, ALU.mult, ALU.add), r, w)

    def memset(out, val, w):
        P.op('dve', lambda e: e.memset(out, val), (), w)

    def dma_sp(out, in_, r, w, sem):
        rec = P.op('sp', lambda e: e.dma_start(out=out, in_=in_), r, w, dma=sem)
        P._last_dma[sem] = rec

    free_slots = list(range(NS))

    def wload(dram_ap, view):
        s = free_slots.pop(0)
        dst = view(slots[s])
        P.op('pool', lambda e: e.dma_start(out=dst, in_=dram_ap), (), [('slot', s)], dma=('slot', s))
        return s

    def wfree(s):
        free_slots.append(s)

    def v3(k):
        return lambda t: t[:, :].rearrange("p (k n) -> p k n", k=k)

    def vec(name, off, n=1):
        o = VOFF[name] + off
        return vecs[:, o:o + n]

    dma_sp(vecs[:, :], d_vecs, (), [('vecs',)], 'in0')
    dma_sp(cst[:, :], d_cst, (), [('cst',)], 'in0b')
    memset(ones_m[:, 0, :], 1.0 / 1024, [('c1',)])
    memset(ones_m[:, 1, :], 1.0 / 512, [('c2',)])
    memset(ones_m[:, 2, :], 1.0 / 256, [('c3',)])
    memset(ones_1[:, :], 1.0, [('c4',)])
    memset(epst[:, :], EPS, [('c5',)])
    memset(stst[:, :], 0.0, [('stst',)])
    act(cT[:, :, :].rearrange("p c k -> p (c k)"), vec('cond', 0, 16), AF.Silu, [('vecs',)], [('cT',)])
    act(rgd[:, 2, :], vec('lam', 0, 32), AF.Exp, [('vecs',)], [('rgd2',)], scale=-1.0)
    act(rgd[:, 3, :], rgd[:, 2, :], AF.Ln, [('rgd2',)], [('rgd3',)], bias=1.0)
    ts(rgd[:, 0, :], rgd[:, 3, :], -8.0, None, ALU.mult, None, [('rgd3',)], [('rgd0',)])
    ts(rgd[:, 1, :], rgd[:, 3, :], -16.0, None, ALU.mult, None, [('rgd3',)], [('rgd1',)])
    P.barrier()

    salloc_reset()
    stg = [salloc([128, 1024], F32) for _ in range(2)]
    for tt_i in range(20):
        src = d_xs[tt_i * 128:(tt_i + 1) * 128, :] if tt_i < 16 else d_xp[(tt_i - 16) * 128:(tt_i - 15) * 128, :]
        sg = stg[tt_i % 2]
        dma_sp(sg[:, :], src, (), [('stg', tt_i % 2)], 'in%d' % (1 + tt_i % 2))
        for half in range(2):
            b = psnext()
            for q in range(4):
                k = half * 4 + q
                tr(psb[b][:, q * 128:(q + 1) * 128], sg[:, k * 128:(k + 1) * 128], ident, [('stg', tt_i % 2)], [('ps', b)])
            dst = xT[:, half * 4:half * 4 + 4, tt_i * 128:(tt_i + 1) * 128]
            srcp = psb[b][:, :].rearrange("p (q n) -> p q n", q=4)
            g = tt_i // 4
            wk = [('x', half * 4 + q, g, tt_i % 4) for q in range(4)]
            if half == 0:
                act(dst, srcp, AF.Copy, [('ps', b)], wk)
            else:
                dcopy(dst, srcp, [('ps', b)], wk)

    def xkeys(k, g):
        return [('x', k, g, i) for i in range(4)]

    ADA_BANK = 7

    def ada_piece(l, nb):
        s = wload(d_ada_w[l].rearrange("(k p) n -> p k n", p=128)[:, :, nb * 512:(nb + 1) * 512], v3(8))
        sv = v3(8)(slots[s])
        for c in range(4):
            n = nb * 4 + c
            for k in range(8):
                mm(psb[ADA_BANK][:, n * 2:n * 2 + 2], sv[:, k, c * 128:(c + 1) * 128], cT[:, :, k],
                   k == 0, k == 7, [('slot', s)], [('ps', ADA_BANK)])
        wfree(s)

    def ada_finish(l):
        pv = psb[ADA_BANK][:, 0:96].rearrange("p (n c) -> p n c", c=2)
        for c in range(2):
            tt(mod[:, l, :, c], pv[:, :, c], vec('ada_b', l * 48, 48), ALU.add, [('ps', ADA_BANK)], [('mod', l, c)])
        for sub in range(2):
            nm = vec('norm_mix' if sub == 0 else 'norm_mlp', l * 8, 8)
            j = 1 if sub == 0 else 4
            for c in range(2):
                stt(AA[:, l, sub, c, :], mod[:, l, j * 8:(j + 1) * 8, c], 1.0, nm, ALU.add, ALU.mult,
                    [('mod', l, c)], [('AA', l, sub, c)])

    def modv(l, j, c, k):
        return mod[:, l, j * 8 + k:j * 8 + k + 1, c]

    ps_pool[0] = list(range(7))
    for nb in range(12):
        ada_piece(0, nb)
    ada_finish(0)
    P.barrier()

    def norm_mod(l, sub, g, dst_fn, dst_keys, sq, rstd, tmps, lnt):
        c = 1 if g < 4 else 0
        cols = slice(g * GS, (g + 1) * GS)
        xk = [k_ for k in range(8) for k_ in xkeys(k, g)]
        act(sq[:, :, :], xT[:, :, cols], AF.Square, xk, [('sq',)])
        b = psnext()
        for k in range(8):
            mm(psb[b][:, :], ones_m[:, 0, :], sq[:, k, :], k == 0, k == 7, [('sq',)], [('ps', b)])
        act(lnt[:, :], psb[b][:, :], AF.Ln, [('ps', b)], [('lnt',)], bias=epst[:, 0:1])
        act(rstd[:, :], lnt[:, :], AF.Exp, [('lnt',)], [('rstd',)], scale=-0.5)
        for k in range(8):
            tm = tmps[k % len(tmps)]
            tk = ('tmp', k % len(tmps))
            stt(tm[:, :], xT[:, k, cols], AA[:, l, sub, c, k:k + 1], rstd[:, :], ALU.mult, ALU.mult,
                xkeys(k, g) + [('rstd',)], [tk])
            act(dst_fn(k), tm[:, :], AF.Identity, [tk], [dst_keys(k)], bias=modv(l, 0 if sub == 0 else 3, c, k))

    def norm_all(l, sub):
        salloc_reset()
        sq = salloc([128, 8, GS], BF16)
        rstd = salloc([128, GS], F32)
        lnt = salloc([128, GS], F32)
        tmps = [salloc([128, GS], F32) for _ in range(3)]
        for g in range(NG):
            norm_mod(l, sub, g, lambda k, g=g: hT[:, k, g * GS:(g + 1) * GS], lambda k, g=g: ('h', k, g), sq, rstd, tmps, lnt)
        P.barrier()

    def mlp(l):
        salloc_reset()
        n_sq = salloc([128, 8, GS], BF16)
        n_rstd = salloc([128, GS], F32)
        n_lnt = salloc([128, GS], F32)
        n_tmps = [salloc([128, GS], F32) for _ in range(3)]

        def nrm(g):
            norm_mod(l, 1, g, lambda k, g=g: hT[:, k, g * GS:(g + 1) * GS], lambda k, g=g: ('h', k, g),
                     n_sq, n_rstd, n_tmps, n_lnt)
        nrm(0)
        nrm(1)
        aT = [salloc([128, 4, GS], BF16) for _ in range(2)]
        rl = [salloc([128, GS], F32) for _ in range(3)]
        ai = 0
        ri = 0
        ada_sched = [2, 1, 2, 1, 2, 1, 2, 1]
        ada_nb = 0
        def ldw(fg_):
            a_ = wload(d_w1[l].rearrange("(k p) n -> p k n", p=128)[:, :, fg_ * 512:(fg_ + 1) * 512], v3(8))
            b_ = wload(d_w2[l][fg_ * 512:(fg_ + 1) * 512, :].rearrange("(k p) n -> p k n", p=128), v3(4))
            return a_, b_
        nxt = ldw(0)
        for fg in range(8):
            s1, s2 = nxt
            if fg < 7:
                nxt = ldw(fg + 1)
            w1v = v3(8)(slots[s1]); w2v = v3(4)(slots[s2])
            for g in range(NG):
                if fg == 0 and g + 2 < NG:
                    nrm(g + 2)
                if l < 3 and g in (1, 3) and (g == 1 or ada_sched[fg] == 2):
                    ada_piece(l + 1, ada_nb); ada_nb += 1
                c = 1 if g < 4 else 0
                cols = slice(g * GS, (g + 1) * GS)
                a = aT[ai % 2]; ak = ai % 2; ai += 1
                for f in range(4):
                    b = psnext()
                    for k in range(8):
                        mm(psb[b][:, :], w1v[:, k, f * 128:(f + 1) * 128], hT[:, k, cols], k == 0, k == 7,
                           [('slot', s1), ('h', k, g)], [('ps', b)])
                    rt = rl[ri % 3]; rk = ('rl', ri % 3); ri += 1
                    act(rt[:, :], psb[b][:, :], AF.Relu, [('ps', b)], [rk])
                    act(a[:, f, :], rt[:, :], AF.Square, [rk], [('aT', ak, f)])
                for n in range(8):
                    b = psnext()
                    for f in range(4):
                        mm(psb[b][:, :], w2v[:, f, n * 128:(n + 1) * 128], a[:, f, :], f == 0, f == 3,
                           [('slot', s2), ('aT', ak, f)], [('ps', b)])
                    stt(xT[:, n, cols], psb[b][:, :], modv(l, 5, c, n), xT[:, n, cols], ALU.mult, ALU.add,
                        [('ps', b)] + xkeys(n, g), xkeys(n, g))
            wfree(s1); wfree(s2)
        if l < 3:
            assert ada_nb == 12, ada_nb
            ada_finish(l + 1)
        P.barrier()

    def rglru(l, j):
        import os
        KPOOL = int(os.environ.get('KPOOL', '1'))
        norm_all(l, 0)
        ps_pool[0] = [4, 5, 6, 7]
        salloc_reset()
        TS = 2048
        b1 = salloc([128, TS], F32)
        gy = salloc([128, TS], BF16)
        xc = salloc([128, TS], F32)
        u_shared = salloc([128, GS], BF16)
        pb = salloc([128, GS], F32)
        sets = [dict(r=salloc([128, GS], F32), a=salloc([128, GS], F32), b=salloc([128, GS], F32),
                     xb=salloc([128, GS], BF16), u=u_shared) for _ in range(2)]
        t_hbs = [salloc([128, GS], F32) for _ in range(2)]
        hbc = salloc([128, 1], F32)
        ui = [0]
        sg_ = wload(d_wa[j].rearrange("d n k j -> k (d n) j"), lambda t: t[:, 0:2048].rearrange("p (m j) -> p m j", j=128))
        sgx = None
        gav = slots[sg_][:, 0:2048].rearrange("p (m j) -> p m j", j=128)
        dstx = slots[sg_][:, 2048:4096].rearrange("p (m j) -> p m j", j=128)
        srcx = d_wx[j].rearrange("d n k j -> k (d n) j")
        P.op('pool', lambda e: e.dma_start(out=dstx, in_=srcx), (), [('slot', sg_)], dma=('slot', sg_))
        gxv = dstx
        for q in range(2):
            sx = wload(d_win[j].rearrange("(k p) n -> p k n", p=128)[:, :, q * 512:(q + 1) * 512], v3(8))
            sy = wload(d_win[j].rearrange("(k p) n -> p k n", p=128)[:, :, 1024 + q * 512:1024 + (q + 1) * 512], v3(8))
            so = wload(d_wout[j][q * 512:(q + 1) * 512, :].rearrange("(k p) n -> p k n", p=128), v3(4))
            wxv = v3(8)(slots[sx]); wyv = v3(8)(slots[sy]); wov = v3(4)(slots[so])
            for nn in range(4):
                n = q * 4 + nn
                cw = lambda tap: vec('conv_w', j * 32 + tap * 8 + n, 1)
                cb = vec('conv_b', j * 8 + n, 1)
                for pas in range(2):
                    if pas == 0:
                        groups = [0, 1, 2, 3]; segs = [(0, 2048, True)]; off = 0
                    else:
                        groups = [4]; segs = [(0, 256, False), (256, 512, False)]; off = 2048
                    nblk = len(groups)
                    for bi, g in enumerate(groups):
                        cols = slice(g * GS, (g + 1) * GS)
                        lc = slice(bi * GS, (bi + 1) * GS)
                        b = psnext()
                        for k in range(8):
                            mm(psb[b][:, :], wxv[:, k, nn * 128:(nn + 1) * 128], hT[:, k, cols], k == 0, k == 7,
                               [('slot', sx), ('h', k, g)], [('ps', b)])
                        act(b1[:, lc], psb[b][:, :], AF.Copy, [('ps', b)], [('b1', bi)])
                        b = psnext()
                        for k in range(8):
                            mm(psb[b][:, :], wyv[:, k, nn * 128:(nn + 1) * 128], hT[:, k, cols], k == 0, k == 7,
                               [('slot', sy), ('h', k, g)], [('ps', b)])
                        act(gy[:, lc], psb[b][:, :], AF.Gelu_apprx_tanh, [('ps', b)], [('gy', bi)])
                    allb1 = [('b1', bi) for bi in range(nblk)]
                    allxc = [('xc', bi) for bi in range(nblk)]
                    for (s0, s1, _) in segs:
                        ts(xc[:, s0:s1], b1[:, s0:s1], cw(1), cb, ALU.mult, ALU.add, allb1 + [('vecs',)], allxc)
                        stt(xc[:, s0 + 1:s1], b1[:, s0:s1 - 1], cw(0), xc[:, s0 + 1:s1], ALU.mult, ALU.add, allb1 + allxc, allxc)
                        stt(xc[:, s0:s1 - 1], b1[:, s0 + 1:s1], cw(2), xc[:, s0:s1 - 1], ALU.mult, ALU.add, allb1 + allxc, allxc)
                        stt(xc[:, s0:s1 - 2], b1[:, s0 + 2:s1], cw(3), xc[:, s0:s1 - 2], ALU.mult, ALU.add, allb1 + allxc, allxc)
                    units = []
                    for (s0, s1, _) in segs:
                        u0 = s0
                        while u0 < s1:
                            units.append((u0, min(u0 + GS, s1), s0, s1))
                            u0 += GS
                    ctxs = []
                    for d in range(2):
                        order = units if d == 0 else units[::-1]
                        for (u0, u1, s0, s1) in order:
                            sk = ui[0] % 2; ui[0] += 1
                            ctxs.append(dict(d=d, u0=u0, u1=u1, s0=s0, s1=s1, W=u1 - u0, bi=u0 // GS, lc=slice(u0, u1),
                                             S=sets[sk], sk=sk))

                    def st_copy_mm(c):
                        S = c['S']; sk = c['sk']; W = c['W']; d = c['d']
                        if KPOOL:
                            P.op('pool', (lambda o_, i_: (lambda e: e.tensor_copy(o_, i_)))(S['xb'][:, 0:W], xc[:, c['lc']]),
                                 allxc, [('t_xb', sk)])
                        else:
                            act(S['xb'][:, 0:W], xc[:, c['lc']], AF.Copy, allxc, [('t_xb', sk)])
                        bA = 2 * sk
                        mm(psb[bA][:, 0:W], gav[:, d * 8 + n, :], S['xb'][:, 0:W], True, True, [('slot', sg_), ('t_xb', sk)], [('ps', bA)])
                        bX = 2 * sk + 1
                        mm(psb[bX][:, 0:W], gxv[:, d * 8 + n, :], S['xb'][:, 0:W], True, True, [('slot', sg_), ('t_xb', sk)], [('ps', bX)])
                        c['bA'] = bA; c['bX'] = bX

                    def st_sig(c):
                        S = c['S']; sk = c['sk']; W = c['W']; d = c['d']
                        act(S['r'][:, 0:W], psb[c['bA']][:, 0:W], AF.Sigmoid, [('ps', c['bA'])], [('t_r', sk)], bias=vec('ba', j * 16 + d * 8 + n, 1))
                        act(S['b'][:, 0:W], psb[c['bX']][:, 0:W], AF.Sigmoid, [('ps', c['bX'])], [('t_b', sk)], bias=vec('bx', j * 16 + d * 8 + n, 1))

                    def st_exp(c):
                        S = c['S']; sk = c['sk']; W = c['W']; d = c['d']
                        col = j * 16 + d * 8 + n
                        kr = ('t_r', sk)
                        act(S['a'][:, 0:W], S['r'][:, 0:W], AF.Exp, [kr], [('t_a', sk)], scale=rgd[:, 0, col:col + 1])
                        act(S['r'][:, 0:W], S['r'][:, 0:W], AF.Exp, [kr], [kr], scale=rgd[:, 1, col:col + 1])
                        act(S['r'][:, 0:W], S['r'][:, 0:W], AF.Ln, [kr], [kr], scale=-1.0, bias=1.0)
                        act(S['r'][:, 0:W], S['r'][:, 0:W], AF.Exp, [kr], [kr], scale=0.5)
                        PE_ = 'pool' if KPOOL else 'dve'
                        P.op(PE_, (lambda o_, a_, b_: (lambda e: e.tensor_tensor(o_, a_, b_, ALU.mult)))(pb[:, 0:W], S['b'][:, 0:W], xc[:, c['lc']]),
                             [('t_b', sk)] + allxc, [('pb',)])
                        P.op(PE_, (lambda o_, a_, b_: (lambda e: e.tensor_tensor(o_, a_, b_, ALU.mult)))(S['b'][:, 0:W], pb[:, 0:W], S['r'][:, 0:W]),
                             [('pb',), kr], [('t_b', sk)])

                    def st_scan(c):
                        S = c['S']; sk = c['sk']; W = c['W']; d = c['d']
                        u0 = c['u0']; u1 = c['u1']; s0 = c['s0']; s1 = c['s1']; bi = c['bi']; lc = c['lc']
                        t_a = S['a']; t_b = S['b']
                        ka = ('t_a', sk); kb = ('t_b', sk)
                        t_hb = t_hbs[sk]; kh = ('t_hb', sk)
                        if d == 0:
                            if u0 == s0:
                                init = vec('state', j * 16 + 0 * 8 + n, 1) if pas == 0 else 0.0
                            else:
                                init = b1[:, u0 - 1:u0]
                            scan(b1[:, lc], t_a[:, 0:W], t_b[:, 0:W], init, [ka, kb] + allb1 + allxc, [('b1', bi)])
                            if pas == 1 and u1 == s1:
                                sq_i = (s0 // 256)
                                c_o = (sq_i * 2 + j) * 16 + 0 * 8 + n
                                dcopy(stst[:, c_o:c_o + 1], b1[:, s1 - 1:s1], [('b1', bi)], [('stst',)])
                        else:
                            if u1 == s1:
                                init = vec('state', j * 16 + 1 * 8 + n, 1) if pas == 0 else 0.0
                            else:
                                init = hbc[:, 0:1]
                            rv = lambda ap, W=W: ap[:, 0:W][:, ::-1]
                            scan(rv(t_hb), rv(t_a), rv(t_b), init, [ka, kb, ('hbc',)], [kh])
                            dcopy(hbc[:, 0:1], t_hb[:, 0:1], [kh], [('hbc',)])
                            if pas == 1 and u0 == s0:
                                sq_i = (s0 // 256)
                                c_o = (sq_i * 2 + j) * 16 + 1 * 8 + n
                                dcopy(stst[:, c_o:c_o + 1], t_hb[:, 0:1], [kh], [('stst',)])

                    def st_comb(c):
                        S = c['S']; sk = c['sk']; W = c['W']; d = c['d']
                        if d == 0:
                            return
                        u0 = c['u0']; bi = c['bi']; lc = c['lc']
                        t_hb = t_hbs[sk]; kh = ('t_hb', sk)
                        t_u = S['u']; ku = ('t_u',)
                        P.op('pool', (lambda o_, a_, b_: (lambda e: e.tensor_tensor(o_, a_, b_, ALU.add)))(pb[:, 0:W], b1[:, lc], t_hb[:, 0:W]),
                             [('b1', bi), kh], [('pb',)])
                        P.op('pool', (lambda o_, a_, b_: (lambda e: e.tensor_tensor(o_, a_, b_, ALU.mult)))(t_u[:, 0:W], pb[:, 0:W], gy[:, lc]),
                             [('pb',), ('gy', bi)], [ku])
                        gcol0 = off + u0
                        g = gcol0 // GS
                        cnd = 1 if g < 4 else 0
                        gc = slice(gcol0, gcol0 + W)
                        for m in range(8):
                            b = psnext()
                            mm(psb[b][:, 0:W], wov[:, nn, m * 128:(m + 1) * 128], t_u[:, 0:W], True, True,
                               [('slot', so), ku], [('ps', b)])
                            stt(xT[:, m, gc], psb[b][:, 0:W], modv(l, 2, cnd, m), xT[:, m, gc], ALU.mult, ALU.add,
                                [('ps', b)] + xkeys(m, g), xkeys(m, g))

                    pairs = [ctxs[i:i + 2] for i in range(0, len(ctxs), 2)]
                    for c in pairs[0]:
                        st_copy_mm(c)
                    for pi_, pr in enumerate(pairs):
                        for c in pr:
                            st_sig(c)
                        if pi_ + 1 < len(pairs):
                            for c in pairs[pi_ + 1]:
                                st_copy_mm(c)
                        for c in pr:
                            st_exp(c)
                        for c in pr:
                            st_scan(c)
                        for c in pr:
                            st_comb(c)
            wfree(sx); wfree(sy); wfree(so)
        wfree(sg_)
        ps_pool[0] = list(range(7))
        P.barrier()


    hbase = [0]

    def mla(l, j):
        import os
        SCALE = 192.0 ** -0.5
        salloc_reset()
        save = cur[0]
        cur[0] = 81920
        cqn = salloc([128, 4, T], BF16)
        ckvT = salloc([128, 2, 2816], BF16)
        kpT = salloc([64, 2816], BF16)
        assert cur[0] <= 81920 + 40960
        cur[0] = save
        hg = salloc([128, 8, GS], BF16)
        sq = salloc([128, 8, GS], BF16)
        rstd = salloc([128, GS], F32)
        lnt = salloc([128, GS], F32)
        tmps = [salloc([128, GS], F32) for _ in range(2)]
        cqf = salloc([128, 6, GS], F32)
        kpf = salloc([64, GS], F32)
        ost = salloc([128, 320], F32)
        s_q = wload(d_dqkv[j].rearrange("(k p) n -> p k n", p=128)[:, :, 0:512], v3(8))
        s_kv = wload(d_dqkv[j].rearrange("(k p) n -> p k n", p=128)[:, :, 512:832],
                     lambda t: t[:, 0:2560].rearrange("p (k n) -> p k n", k=8))
        wqv = v3(8)(slots[s_q])
        wkvv = slots[s_kv][:, 0:2560].rearrange("p (k n) -> p k n", k=8)
        ev = [0]

        def evac(dst, src, r, w, scale=None):
            ev[0] += 1
            if ev[0] % 2 == 0 or scale is not None:
                act(dst, src, AF.Copy, r, w, scale=scale)
            else:
                dcopy(dst, src, r, w)

        def rope(src, g, dst, rk, wk, Cg, Sg, t1, t2, ck, sk, k1, k2):
            cols = slice(g * GS, (g + 1) * GS)
            dma_sp(Cg[0:64, :], d_ropec[:, cols], (), [ck], 'ropeC')
            dma_sp(Sg[0:64, :], d_ropes[:, cols], (), [sk], 'ropeS')
            b = psnext()
            mm(psb[b][0:64, :], Rm, src, True, True, rk, [('ps', b)])
            tt(t1[0:64, :], src, Cg[0:64, :], ALU.mult, rk + [ck], [k1])
            tt(t2[0:64, :], psb[b][0:64, :], Sg[0:64, :], ALU.mult, [('ps', b), sk], [k2])
            tt(dst, t1[0:64, :], t2[0:64, :], ALU.add, [k1, k2], wk)

        for g in range(NG):
            cols = slice(g * GS, (g + 1) * GS)
            norm_mod(l, 0, g, lambda k: hg[:, k, :], lambda k: ('hg', k), sq, rstd, tmps, lnt)
            hk = [('hg', k) for k in range(8)]
            KP2 = int(os.environ.get('KP2', '9'))
            if KP2 < 1:
                continue
            for m in range(6 if KP2 != 1 else 4):
                b = psnext()
                for k in range(8):
                    lw = wqv[:, k, m * 128:(m + 1) * 128] if m < 4 else wkvv[:, k, (m - 4) * 128:(m - 3) * 128]
                    mm(psb[b][:, :], lw, hg[:, k, :], k == 0, k == 7, [('slot', s_q if m < 4 else s_kv), ('hg', k)], [('ps', b)])
                evac(cqf[:, m, :], psb[b][:, :], [('ps', b)], [('cqf', m)])
            if KP2 == 1:
                continue
            b = psnext()
            for k in range(8):
                mm(psb[b][0:64, :], wkvv[:, k, 256:320], hg[:, k, :], k == 0, k == 7, [('slot', s_kv), ('hg', k)], [('ps', b)])
            evac(kpf[:, :], psb[b][0:64, :], [('ps', b)], [('kpf',)])
            if KP2 < 2:
                continue
            act(sq[:, 0:6, :], cqf[:, :, :], AF.Square, [('cqf', m) for m in range(6)], [('sq',)])
            for (m0, m1, oi, nm, nbase) in ((0, 4, 1, 'normq', j * 4), (4, 6, 2, 'normkv', j * 2)):
                b = psnext()
                for m in range(m0, m1):
                    mm(psb[b][:, :], ones_m[:, oi, :], sq[:, m, :], m == m0, m == m1 - 1, [('sq',)], [('ps', b)])
                act(lnt[:, :], psb[b][:, :], AF.Ln, [('ps', b)], [('lnt',)], bias=epst[:, 0:1])
                act(rstd[:, :], lnt[:, :], AF.Exp, [('lnt',)], [('rstd',)], scale=-0.5)
                for m in range(m0, m1):
                    gam = vec(nm, nbase + (m - m0), 1)
                    if m < 4:
                        stt(cqn[:, m, cols], cqf[:, m, :], gam, rstd[:, :], ALU.mult, ALU.mult,
                            [('cqf', m), ('rstd',)], [('cqn', m, g)])
                    elif g < 4:
                        stt(ckvT[:, m - 4, cols], cqf[:, m, :], gam, rstd[:, :], ALU.mult, ALU.mult,
                            [('cqf', m), ('rstd',)], [('ckv', g * 4 + i) for i in range(4)])
                    else:
                        stt(cqf[:, m, :], cqf[:, m, :], gam, rstd[:, :], ALU.mult, ALU.mult,
                            [('cqf', m), ('rstd',)], [('cqf', m)])
                        act(ckvT[:, m - 4, 2304:2816], cqf[:, m, :], AF.Copy, [('cqf', m)], [('ckv', 18 + i) for i in range(4)])
            if KP2 < 3:
                continue
            if g < 4:
                rope(kpf[:, :], g, kpT[:, cols], [('kpf',)], [('kp', g * 4 + i) for i in range(4)],
                     lnt, rstd, tmps[0], tmps[1], ('lnt',), ('rstd',), ('tmp', 0), ('tmp', 1))
            else:
                act(kpT[:, 2304:2816], kpf[:, :], AF.Copy, [('kpf',)], [('kp', 18 + i) for i in range(4)])
                for t4 in range(4 if KP2 > 3 else 0):
                    b = psnext()
                    for m in range(2):
                        tr(psb[b][:, m * 128:(m + 1) * 128], cqf[:, 4 + m, t4 * 128:(t4 + 1) * 128], ident, [('cqf', 4 + m)], [('ps', b)])
                    tr(psb[b][:, 256:320], kpf[:, t4 * 128:(t4 + 1) * 128], ident[0:64, 0:64], [('kpf',)], [('ps', b)])
                    act(ost[:, 0:320], psb[b][:, 0:320], AF.Copy, [('ps', b)], [('ost',)])
                    sqi = t4 // 2
                    r0 = (t4 % 2) * 128
                    if KP2 != 5:
                        dma_sp(o_ckv[sqi, j, r0:r0 + 128, :], ost[:, 0:256], [('ost',)], [], 'out3')
                    if KP2 != 5 and KP2 != 6:
                        dma_sp(o_kpe[sqi, j, r0:r0 + 128, :], ost[:, 256:320], [('ost',)], [], 'out4')
        wfree(s_q); wfree(s_kv)
        P.barrier()

        import os
        if os.environ.get('KSKIP_P3'):
            return
        salloc_reset()
        qn = salloc([128, T], BF16)
        qp = salloc([64, T], BF16)
        kn = salloc([128, 2816], BF16)
        V = salloc([128, 22, 128], BF16)
        pTs = [salloc([128, GS], BF16) for _ in range(3)]
        oTs = [salloc([128, GS], BF16) for _ in range(3)]
        rz = salloc([128, GS], F32)
        qf = salloc([64, GS], F32)
        Cg = salloc([64, GS], F32); Sg = salloc([64, GS], F32)
        save3 = cur[0]
        cst1 = salloc([128, 2, 256], F32)
        cst2 = salloc([128, 2, 64], F32)
        cur[0] = save3
        t1 = salloc([64, GS], F32); t2 = salloc([64, GS], F32)
        dma_sp(cst1[:, :, :], d_cckv[j].rearrange("(t p) f -> p t f", p=128), (), [('cst1',)], 'in1')
        dma_sp(cst2[:, :, :], d_ckpe[j].rearrange("(t p) f -> p t f", p=128), (), [('cst2',)], 'in2')
        ps_pool[0] = [0, 1, 2, 3]
        b = psnext()
        for m in range(2):
            for t in range(2):
                tr(psb[b][:, (m * 2 + t) * 128:(m * 2 + t + 1) * 128], cst1[:, t, m * 128:(m + 1) * 128], ident, [('cst1',)], [('ps', b)])
        dcopy(ckvT[:, :, 2048:2304], psb[b][:, :].rearrange("p (m n) -> p m n", m=2), [('ps', b)], [('ckv', 16), ('ckv', 17)])
        b = psnext()
        for t in range(2):
            tr(psb[b][0:64, t * 128:(t + 1) * 128], cst2[:, t, :], ident, [('cst2',)], [('ps', b)])
        act(kpT[:, 2048:2304], psb[b][0:64, 0:256], AF.Copy, [('ps', b)], [('kp', 16), ('kp', 17)])

        s_uq = [wload(d_uq[j].rearrange("(k p) n -> p k n", p=128)[:, :, q * 768:(q + 1) * 768],
                      lambda t: t[:, 0:3072].rearrange("p (k n) -> p k n", k=4)) for q in range(2)]
        s_ukv = wload(d_ukv[j].rearrange("(k p) n -> p k n", p=128), v3(2))
        s_wo = [wload(d_wo[j].rearrange("(h p) n -> p h n", p=128)[:, :, q * 512:(q + 1) * 512], v3(8)) for q in range(2)]
        ukvv = v3(2)(slots[s_ukv])
        wov = [v3(8)(slots[s]) for s in s_wo]
        oz = [0]
        pi = [0]
        oi_ = [0]
        allckv = [('ckv', i) for i in range(22)]
        allkp = [('kp', i) for i in range(22)]
        for h in range(8):
            uqv = slots[s_uq[h // 4]][:, 0:3072].rearrange("p (k n) -> p k n", k=4)
            su = s_uq[h // 4]
            off = (h % 4) * 192
            for g in range(NG):
                cols = slice(g * GS, (g + 1) * GS)
                b = psnext()
                for k in range(4):
                    mm(psb[b][:, :], uqv[:, k, off:off + 128], cqn[:, k, cols], k == 0, k == 3, [('slot', su), ('cqn', k, g)], [('ps', b)])
                act(qn[:, cols], psb[b][:, :], AF.Copy, [('ps', b)], [('qn', g)], scale=SCALE)
                b = psnext()
                for k in range(4):
                    mm(psb[b][0:64, :], uqv[:, k, off + 128:off + 192], cqn[:, k, cols], k == 0, k == 3, [('slot', su), ('cqn', k, g)], [('ps', b)])
                if g < 4:
                    act(qf[:, :], psb[b][0:64, :], AF.Copy, [('ps', b)], [('qf',)], scale=SCALE)
                    rope(qf[:, :], g, qp[:, cols], [('qf',)], [('qp', g)], Cg, Sg, t1, t2, ('Cg',), ('Sg',), ('t1',), ('t2',))
                else:
                    act(qp[:, cols], psb[b][0:64, :], AF.Copy, [('ps', b)], [('qp', g)], scale=SCALE)
            for blk in range(6):
                c0 = blk * GS
                W = min(GS, 2816 - c0)
                b = psnext()
                for c in range(2):
                    mm(psb[b][:, 0:W], ukvv[:, c, h * 256:h * 256 + 128], ckvT[:, c, c0:c0 + W], c == 0, c == 1,
                       [('slot', s_ukv)] + [('ckv', blk * 4 + i) for i in range(W // 128)], [('ps', b)])
                evac(kn[:, c0:c0 + W], psb[b][:, 0:W], [('ps', b)], [('kn', blk)])
            for t0 in range(0, 22, 4):
                nt = min(4, 22 - t0)
                b = psnext()
                for q in range(nt):
                    tt_ = t0 + q
                    for c in range(2):
                        mm(psb[b][:, q * 128:(q + 1) * 128], ckvT[:, c, tt_ * 128:(tt_ + 1) * 128], ukvv[:, c, h * 256 + 128:h * 256 + 256],
                           c == 0, c == 1, [('slot', s_ukv), ('ckv', tt_)], [('ps', b)])
                evac(V[:, t0:t0 + nt, :], psb[b][:, 0:nt * 128].rearrange("p (q n) -> p q n", q=nt), [('ps', b)], [('V', t0 // 4)])
            jobs = [(qg * GS, GS, list(range(18)), qg, 1) for qg in range(4)]
            jobs += [(2048 + s_ * 256, 256, [18 + 2 * s_, 19 + 2 * s_], 4, 0) for s_ in range(2)]
            LAG = 2
            steps = []
            for ji, (q0, W, kts, g, cnd) in enumerate(jobs):
                for ki, kt in enumerate(kts):
                    steps.append((ji, ki, kt))
            jbank = {}
            pend = []
            fin_pe = []

            def emit_S(ji, ki, kt):
                (q0, W, kts, g, cnd) = jobs[ji]
                if ki == 0:
                    jbank[ji] = 4 + 2 * (oz[0] % 2); oz[0] += 1
                qc = slice(q0, q0 + W)
                kc = slice(kt * 128, (kt + 1) * 128)
                b = psnext()
                mm(psb[b][:, 0:W], kn[:, kc], qn[:, qc], True, False, [('kn', kt // 4), ('qn', g)], [('ps', b)])
                mm(psb[b][:, 0:W], kpT[:, kc], qp[:, qc], False, True, [('kp', kt), ('qp', g)], [('ps', b)])
                pT = pTs[pi[0] % 3]; pk = ('pT', pi[0] % 3); pi[0] += 1
                act(pT[:, 0:W], psb[b][:, 0:W], AF.Exp, [('ps', b)], [pk])
                return (pT, pk)

            def emit_PV(ji, ki, kt, pT, pk):
                (q0, W, kts, g, cnd) = jobs[ji]
                bO = jbank[ji]; bZ = bO + 1
                last = ki == len(kts) - 1
                mm(psb[bO][:, 0:W], V[:, kt, :], pT[:, 0:W], ki == 0, last, [('V', kt // 4), pk], [('ps', bO)])
                mm(psb[bZ][:, 0:W], ones_1[:, :], pT[:, 0:W], ki == 0, last, [pk], [('ps', bZ)])
                if last:
                    qc = slice(q0, q0 + W)
                    recip(rz[:, 0:W], psb[bZ][:, 0:W], [('ps', bZ)], [('rz',)])
                    oT = oTs[oi_[0] % 3]; ok = ('oT', oi_[0] % 3); oi_[0] += 1
                    tt(oT[:, 0:W], psb[bO][:, 0:W], rz[:, 0:W], ALU.mult, [('ps', bO), ('rz',)], [ok])

                    def fin(W=W, qc=qc, g=g, cnd=cnd, oT=oT, ok=ok):
                        for m in range(8):
                            b = psnext()
                            mm(psb[b][:, 0:W], wov[m // 4][:, h, (m % 4) * 128:(m % 4 + 1) * 128], oT[:, 0:W], True, True,
                               [('slot', s_wo[m // 4]), ok], [('ps', b)])
                            xk = xkeys(m, g)
                            stt(xT[:, m, qc], psb[b][:, 0:W], modv(l, 2, cnd, m), xT[:, m, qc], ALU.mult, ALU.add,
                                [('ps', b)] + xk, xk)
                    fin_pe.append([LAG + 1, fin])

            def tick_fin(force=False):
                for it in list(fin_pe):
                    it[0] -= 1
                    if it[0] <= 0 or force:
                        it[1]()
                        fin_pe.remove(it)

            for (ji, ki, kt) in steps:
                pT, pk = emit_S(ji, ki, kt)
                pend.append((ji, ki, kt, pT, pk))
                tick_fin()
                if len(pend) > LAG:
                    emit_PV(*pend.pop(0))
            while pend:
                tick_fin()
                emit_PV(*pend.pop(0))
            tick_fin(force=True)
        for s_ in s_uq + [s_ukv] + s_wo:
            wfree(s_)
        ps_pool[0] = list(range(7))
        P.barrier()

    def final_out():
        salloc_reset()
        sq = salloc([128, 8, GS], BF16)
        rstd = salloc([128, GS], F32)
        lnt = salloc([128, GS], F32)
        yT = salloc([128, 8, GS], F32)
        ostg = [salloc([128, 1024], F32) for _ in range(2)]
        oi = 0
        for g in range(NG):
            cols = slice(g * GS, (g + 1) * GS)
            xk = [k_ for k in range(8) for k_ in xkeys(k, g)]
            act(sq[:, :, :], xT[:, :, cols], AF.Square, xk, [('sq',)])
            b = psnext()
            for k in range(8):
                mm(psb[b][:, :], ones_m[:, 0, :], sq[:, k, :], k == 0, k == 7, [('sq',)], [('ps', b)])
            act(lnt[:, :], psb[b][:, :], AF.Ln, [('ps', b)], [('lnt',)], bias=epst[:, 0:1])
            act(rstd[:, :], lnt[:, :], AF.Exp, [('lnt',)], [('rstd',)], scale=-0.5)
            for k in range(8):
                stt(yT[:, k, :], xT[:, k, cols], vec('final', k, 1), rstd[:, :], ALU.mult, ALU.mult,
                    xkeys(k, g) + [('rstd',)], [('yT', k)])
            for t4 in range(4):
                og = ostg[oi % 2]; ok = ('ostg', oi % 2)
                for half in range(2):
                    b = psnext()
                    for q in range(4):
                        k = half * 4 + q
                        tr(psb[b][:, q * 128:(q + 1) * 128], yT[:, k, t4 * 128:(t4 + 1) * 128], ident, [('yT', k)], [('ps', b)])
                    if half == 0:
                        act(og[:, 0:512], psb[b][:, :], AF.Copy, [('ps', b)], [(ok, 0)])
                    else:
                        dcopy(og[:, 512:1024], psb[b][:, :], [('ps', b)], [(ok, 1)])
                row0 = g * GS + t4 * 128
                dst = o_ys[row0:row0 + 128, :] if g < 4 else o_yp[row0 - 2048:row0 - 2048 + 128, :]
                dma_sp(dst, og[:, :], [(ok, 0), (ok, 1)], [], 'out%d' % (oi % 2))
                oi += 1
        b = psnext()
        tr(psb[b][0:64, 0:128], stst[:, :], ident, [('stst',)], [('ps', b)])
        so_ = salloc([64, 128], F32)
        dcopy(so_[:, :], psb[b][0:64, 0:128], [('ps', b)], [('so_',)])
        dma_sp(o_st, so_[:, :], [('so_',)], [], 'out2')

    import os
    steps = os.environ.get('KSTEPS', 'full')
    if steps == 'full':
        for l in range(4):
            j = l // 2
            if l % 2 == 0:
                rglru(l, j)
            else:
                mla(l, j)
            mlp(l)
    else:
        for st_ in steps.split(','):
            if st_ == 'mla':
                mla(0, 0)
            elif st_ == 'rg':
                rglru(0, 0)
            elif st_ == 'mlp':
                mlp(0)
    final_out()

    P.emit(['out0', 'out1', 'out2', 'out3', 'out4'])
    return nc


_CACHE = {}


def rope_tables():
    n = 2048
    row = np.repeat(np.arange(n // 64, dtype=np.float32), 64)
    col = np.tile(np.arange(64, dtype=np.float32), n // 64)
    half = 32
    inv = (10000.0 ** (-np.arange(0, half, 2, dtype=np.float32) / half)).astype(np.float32)
    ang = np.concatenate([row[:, None] * inv, col[:, None] * inv], axis=-1)
    c = np.cos(ang).astype(np.float32); s = np.sin(ang).astype(np.float32)
    C = np.repeat(c, 2, axis=1).T
    S = np.repeat(s, 2, axis=1).T
    return np.ascontiguousarray(C), np.ascontiguousarray(S)


def kernel(**inp):
    f32 = lambda a: np.ascontiguousarray(np.asarray(a, dtype=np.float32))
    if 'nc' not in _CACHE:
        _CACHE['nc'] = build_program()
    nc = _CACHE['nc']
    C, S = rope_tables()
    cstm = np.zeros((128, 192), np.float32)
    cstm[:, 0:128] = np.eye(128, dtype=np.float32)
    for i in range(32):
        cstm[2 * i + 1, 128 + 2 * i] = -1.0
        cstm[2 * i, 128 + 2 * i + 1] = 1.0
    shared = {k: f32(inp[k]) for k in ['ada_w', 'mlp_w1', 'mlp_w2', 'rg_w_in', 'rg_wa', 'rg_wx', 'rg_w_out',
                                      'mla_w_dqkv', 'mla_w_uq', 'mla_w_ukv', 'mla_w_o']}
    in_maps = []
    for c in range(NCORE):
        parts = {
            'ada_b': fm(inp['ada_b']), 'norm_mix': fm(inp['norm_mix']), 'norm_mlp': fm(inp['norm_mlp']),
            'final': fm(inp['final_norm']), 'conv_w': fm(inp['rg_conv_w']), 'conv_b': fm(inp['rg_conv_b']),
            'ba': fm(inp['rg_ba']), 'bx': fm(inp['rg_bx']), 'lam': fm(inp['rg_lambda']),
            'normq': fm(inp['mla_norm_q']), 'normkv': fm(inp['mla_norm_kv']),
            'state': fm(np.asarray(inp['state_rglru'])[c]),
            'cond': fm(np.stack([np.asarray(inp['c_ctx']), np.asarray(inp['c'])[c]], 0)),
        }
        vecs = np.concatenate([parts[n] for n, _ in VEC_SPEC], axis=1)
        assert vecs.shape == (128, NV), vecs.shape
        m = dict(shared)
        m.update({
            'xs': f32(np.asarray(inp['x_sample'])[c]),
            'xp': f32(np.asarray(inp['x_prompt'])[2 * c:2 * c + 2].reshape(512, 1024)),
            'cckv': f32(np.asarray(inp['cache_ckv'])[c]), 'ckpe': f32(np.asarray(inp['cache_kpe'])[c]),
            'vecs': f32(vecs), 'cst': cstm, 'ropec': C, 'ropes': S,
        })
        in_maps.append(m)
    res = run_bass_kernel_spmd(nc, in_maps, core_ids=list(range(NCORE)))
    R = res.results
    y_prompt = np.concatenate([R[c]['yp'].reshape(2, 256, 1024) for c in range(NCORE)], 0)
    y_sample = np.stack([R[c]['ys'] for c in range(NCORE)], 0)
    nstate = np.concatenate([R[c]['nstate'].reshape(2, 2, 2, 1024) for c in range(NCORE)], 0)
    nckv = np.concatenate([R[c]['nckv'] for c in range(NCORE)], 0)
    nkpe = np.concatenate([R[c]['nkpe'] for c in range(NCORE)], 0)
    return (y_prompt.astype(np.float32), y_sample.astype(np.float32), nstate.astype(np.float32),
            nckv.astype(np.float32), nkpe.astype(np.float32))
```

```python
import contextlib
import numpy as np
import concourse.bass as bass
import concourse.mybir as mybir
from concourse.bass_utils import run_bass_kernel_spmd

F32 = mybir.dt.float32
BF16 = mybir.dt.bfloat16
AF = mybir.ActivationFunctionType
ALU = mybir.AluOpType

NCORE = 8
T = 2560
NG = 5
GS = 512
EPS = 1e-6
SAME_ENG_SYNC = True


class Prog:
    def __init__(self, nc):
        self.nc = nc
        self.engs = ['pe', 'act', 'dve', 'pool', 'sp']
        self.ops = {e: [] for e in self.engs}
        self.st = {}
        self.dma_cnt = {}
        self.pend = {e: [] for e in self.engs}

    def op(self, eng, fn, r=(), w=(), dma=None):
        deps = self.pend[eng]
        self.pend[eng] = []
        for k in r:
            s = self.st.get(k)
            if s is not None and s[0] is not None:
                deps.append(s[0])
        for k in w:
            s = self.st.get(k)
            if s is not None:
                if s[0] is not None:
                    deps.append(s[0])
                deps.extend(s[1])
        rec = {'eng': eng, 'fn': fn, 'deps': deps, 'sig': False, 'dma': None, 'tick': 0}
        if dma is not None:
            c = self.dma_cnt.get(dma, 0) + 1
            self.dma_cnt[dma] = c
            rec['dma'] = (dma, 16 * c)
        self.ops[eng].append(rec)
        for k in r:
            s = self.st.get(k)
            if s is None:
                self.st[k] = [None, [rec]]
            else:
                s[1].append(rec)
        for k in w:
            self.st[k] = [rec, []]
        return rec

    def barrier(self):
        lasts = []
        for e in ('pe', 'act', 'dve', 'sp'):
            if self.ops[e]:
                lasts.append(self.ops[e][-1])
        for e in ('pe', 'act', 'dve', 'sp'):
            self.pend[e] = list(self.pend[e]) + lasts + list(self._last_dma.values())

    _last_dma = {}

    def _need(self, d, rec):
        if d is rec:
            return False
        if d['dma'] is not None:
            return True
        if d['eng'] == rec['eng']:
            if d['eng'] == 'pe':
                return False
            return SAME_ENG_SYNC
        return True

    def emit(self, final_sems):
        nc = self.nc
        for e in self.engs:
            for i, rec in enumerate(self.ops[e]):
                rec['idx'] = i
        for e in self.engs:
            for rec in self.ops[e]:
                best = {}
                dm = {}
                for d in rec['deps']:
                    if not self._need(d, rec):
                        continue
                    if d['dma'] is not None:
                        k = d['dma'][0]
                        if k not in dm or dm[k]['dma'][1] < d['dma'][1]:
                            dm[k] = d
                    else:
                        k = d['eng']
                        if k not in best or best[k]['idx'] < d['idx']:
                            best[k] = d
                for d in best.values():
                    d['sig'] = True
                rec['deps'] = list(best.values()) + list(dm.values())
        for e in self.engs:
            c = 0
            for rec in self.ops[e]:
                if rec['sig'] and rec['dma'] is None:
                    c += 1
                    rec['tick'] = c
        import os
        if os.environ.get('KDEBUG'):
            print('ops', {e: len(self.ops[e]) for e in self.engs})
            print('ticks', {e: max([r['tick'] for r in self.ops[e]] + [0]) for e in self.engs})
            print('dma', {k: 16 * v for k, v in self.dma_cnt.items()})
        with contextlib.ExitStack() as es:
            sems = {e: es.enter_context(nc.semaphore('s_' + e)) for e in self.engs}
            dsems = {k: es.enter_context(nc.semaphore('d_' + str(k))) for k in self.dma_cnt}
            block = es.enter_context(nc.Block())

            def run(e, eo):
                waited = {}
                for rec in self.ops[e]:
                    for d in rec['deps']:
                        if not self._need(d, rec):
                            continue
                        if d['dma'] is not None:
                            key = ('d', d['dma'][0]); val = d['dma'][1]; sem = dsems[d['dma'][0]]
                        else:
                            key = ('e', d['eng']); val = d['tick']; sem = sems[d['eng']]
                        if waited.get(key, 0) < val:
                            eo.wait_ge(sem, val)
                            waited[key] = val
                    ins = rec['fn'](eo)
                    if rec['dma'] is not None:
                        ins.then_inc(dsems[rec['dma'][0]], 16)
                    elif rec['sig']:
                        ins.then_inc(sems[e], 1)
                if e == 'sp':
                    for k in final_sems:
                        if k in self.dma_cnt:
                            eo.wait_ge(dsems[k], 16 * self.dma_cnt[k])

            block.tensor(lambda eo: run('pe', eo))
            block.scalar(lambda eo: run('act', eo))
            block.vector(lambda eo: run('dve', eo))
            block.gpsimd(lambda eo: run('pool', eo))
            block.sync(lambda eo: run('sp', eo))


def fm(v):
    v = np.asarray(v, np.float32)
    F = v.shape[-1]
    a = v.reshape(-1, F // 128, 128)
    return np.ascontiguousarray(a.transpose(2, 0, 1).reshape(128, -1))


VEC_SPEC = [
    ('ada_b', 192), ('norm_mix', 32), ('norm_mlp', 32), ('final', 8), ('conv_w', 64), ('conv_b', 16),
    ('ba', 32), ('bx', 32), ('lam', 32), ('normq', 8), ('normkv', 4), ('state', 32), ('cond', 16),
]
VOFF = {}
_o = 0
for _n, _c in VEC_SPEC:
    VOFF[_n] = _o
    _o += _c
NV = _o


def build_program(debug=False):
    nc = bass.Bass("TRN2", target_bir_lowering=False)
    P = Prog(nc)

    def din(name, shape):
        return nc.dram_tensor(name, list(shape), F32, kind="ExternalInput").ap()

    def dout(name, shape):
        return nc.dram_tensor(name, list(shape), F32, kind="ExternalOutput").ap()

    d_xs = din('xs', (2048, 1024)); d_xp = din('xp', (512, 1024))
    d_cckv = din('cckv', (2, 256, 256)); d_ckpe = din('ckpe', (2, 256, 64))
    d_vecs = din('vecs', (128, NV)); d_cst = din('cst', (128, 192))
    d_ropec = din('ropec', (64, 2048)); d_ropes = din('ropes', (64, 2048))
    d_ada_w = din('ada_w', (4, 1024, 6144))
    d_w1 = din('mlp_w1', (4, 1024, 4096)); d_w2 = din('mlp_w2', (4, 4096, 1024))
    d_win = din('rg_w_in', (2, 1024, 2048)); d_wa = din('rg_wa', (2, 2, 8, 128, 128)); d_wx = din('rg_wx', (2, 2, 8, 128, 128))
    d_wout = din('rg_w_out', (2, 1024, 1024))
    d_dqkv = din('mla_w_dqkv', (2, 1024, 832)); d_uq = din('mla_w_uq', (2, 512, 1536))
    d_ukv = din('mla_w_ukv', (2, 256, 2048)); d_wo = din('mla_w_o', (2, 1024, 1024))
    o_yp = dout('yp', (512, 1024)); o_ys = dout('ys', (2048, 1024))
    o_st = dout('nstate', (64, 128)); o_ckv = dout('nckv', (2, 2, 256, 256)); o_kpe = dout('nkpe', (2, 2, 256, 64))

    ARENA = 212800
    arena = nc.alloc_sbuf_tensor('arena', [128, ARENA // 4], F32)
    base = nc.lookup_mloc(arena).addr
    cur = [0]

    def alloc(name, shape, dt):
        nbytes = int(np.prod(shape[1:])) * (4 if dt == F32 else 2)
        nbytes = (nbytes + 31) // 32 * 32
        t = nc.alloc_sbuf_tensor_at(name, list(shape), dt, offset=base + cur[0])
        cur[0] += nbytes
        assert cur[0] <= ARENA, (name, cur[0])
        return t

    xT = alloc('xT', [128, 8, T], F32)
    hT = alloc('hT', [128, 8, T], BF16)
    NS = 5
    slots = [alloc('slot%d' % i, [128, 4096], BF16) for i in range(NS)]
    vecs = alloc('vecs', [128, NV], F32)
    cst = alloc('cst', [128, 192], F32)
    mod = alloc('mod', [128, 4, 48, 2], F32)
    AA = alloc('AA', [128, 4, 2, 2, 8], F32)
    rgd = alloc('rgd', [128, 4, 32], F32)
    ones_m = alloc('ones_m', [128, 3, 128], BF16)
    ones_1 = alloc('ones_1', [128, 128], BF16)
    cT = alloc('cT', [128, 2, 8], BF16)
    stst = alloc('stst', [128, 64], F32)
    epst = alloc('epst', [128, 1], F32)
    scr_base = cur[0]
    SCR = ARENA - scr_base

    scr_names = [0]

    def salloc_reset():
        cur[0] = scr_base

    def salloc(shape, dt):
        scr_names[0] += 1
        return alloc('scr%d' % scr_names[0], shape, dt)

    ident = cst[:, 0:128]
    Rm = cst[0:64, 128:192]

    psb = [nc.alloc_psum_tensor('ps%d' % i, [128, 512], F32) for i in range(8)]
    ps_pool = [list(range(8))]
    ps_i = [0]

    def psnext():
        b = ps_pool[0][ps_i[0] % len(ps_pool[0])]
        ps_i[0] += 1
        return b

    def mm(out, lhsT, rhs, start, stop, r, w):
        P.op('pe', lambda e: e.matmul(out, lhsT, rhs, start=start, stop=stop), r, w)

    def tr(out, in_, idn, r, w):
        P.op('pe', lambda e: e.transpose(out, in_, idn), r, w)

    def act(out, in_, func, r, w, bias=None, scale=None):
        kw = {}
        if bias is not None:
            kw['bias'] = bias
        if scale is not None:
            kw['scale'] = scale
        P.op('act', lambda e: e.activation(out, in_, func, **kw), r, w)

    def ts(out, in0, s1, s2, op0, op1, r, w):
        if s2 is None:
            P.op('dve', lambda e: e.tensor_scalar(out, in0, s1, None, op0), r, w)
        else:
            P.op('dve', lambda e: e.tensor_scalar(out, in0, s1, s2, op0, op1), r, w)

    def stt(out, in0, sc, in1, op0, op1, r, w):
        P.op('dve', lambda e: e.scalar_tensor_tensor(out, in0, sc, in1, op0, op1), r, w)

    def tt(out, in0, in1, op, r, w):
        P.op('dve', lambda e: e.tensor_tensor(out, in0, in1, op), r, w)

    def dcopy(out, in_, r, w):
        P.op('dve', lambda e: e.tensor_scalar(out, in_, 1.0, None, ALU.mult), r, w)

    def recip(out, in_, r, w):
        P.op('dve', lambda e: e.reciprocal(out, in_), r, w)

    def scan(out, a, b, init, r, w):
        P.op('dve', lambda e: e.tensor_tensor_scan(out, a, b, init, ALU.mult, ALU.add), r, w)

    def memset(out, val, w):
        P.op('dve', lambda e: e.memset(out, val), (), w)

    def dma_sp(out, in_, r, w, sem):
        rec = P.op('sp', lambda e: e.dma_start(out=out, in_=in_), r, w, dma=sem)
        P._last_dma[sem] = rec

    free_slots = list(range(NS))

    def wload(dram_ap, view):
        s = free_slots.pop(0)
        dst = view(slots[s])
        P.op('pool', lambda e: e.dma_start(out=dst, in_=dram_ap), (), [('slot', s)], dma=('slot', s))
        return s

    def wfree(s):
        free_slots.append(s)

    def v3(k):
        return lambda t: t[:, :].rearrange("p (k n) -> p k n", k=k)

    def vec(name, off, n=1):
        o = VOFF[name] + off
        return vecs[:, o:o + n]

    dma_sp(vecs[:, :], d_vecs, (), [('vecs',)], 'in0')
    dma_sp(cst[:, :], d_cst, (), [('cst',)], 'in0b')
    memset(ones_m[:, 0, :], 1.0 / 1024, [('c1',)])
    memset(ones_m[:, 1, :], 1.0 / 512, [('c2',)])
    memset(ones_m[:, 2, :], 1.0 / 256, [('c3',)])
    memset(ones_1[:, :], 1.0, [('c4',)])
    memset(epst[:, :], EPS, [('c5',)])
    memset(stst[:, :], 0.0, [('stst',)])
    act(cT[:, :, :].rearrange("p c k -> p (c k)"), vec('cond', 0, 16), AF.Silu, [('vecs',)], [('cT',)])
    act(rgd[:, 2, :], vec('lam', 0, 32), AF.Exp, [('vecs',)], [('rgd2',)], scale=-1.0)
    act(rgd[:, 3, :], rgd[:, 2, :], AF.Ln, [('rgd2',)], [('rgd3',)], bias=1.0)
    ts(rgd[:, 0, :], rgd[:, 3, :], -8.0, None, ALU.mult, None, [('rgd3',)], [('rgd0',)])
    ts(rgd[:, 1, :], rgd[:, 3, :], -16.0, None, ALU.mult, None, [('rgd3',)], [('rgd1',)])
    P.barrier()

    salloc_reset()
    stg = [salloc([128, 1024], F32) for _ in range(2)]
    for tt_i in range(20):
        src = d_xs[tt_i * 128:(tt_i + 1) * 128, :] if tt_i < 16 else d_xp[(tt_i - 16) * 128:(tt_i - 15) * 128, :]
        sg = stg[tt_i % 2]
        dma_sp(sg[:, :], src, (), [('stg', tt_i % 2)], 'in%d' % (1 + tt_i % 2))
        for half in range(2):
            b = psnext()
            for q in range(4):
                k = half * 4 + q
                tr(psb[b][:, q * 128:(q + 1) * 128], sg[:, k * 128:(k + 1) * 128], ident, [('stg', tt_i % 2)], [('ps', b)])
            dst = xT[:, half * 4:half * 4 + 4, tt_i * 128:(tt_i + 1) * 128]
            srcp = psb[b][:, :].rearrange("p (q n) -> p q n", q=4)
            g = tt_i // 4
            wk = [('x', half * 4 + q, g, tt_i % 4) for q in range(4)]
            if half == 0:
                act(dst, srcp, AF.Copy, [('ps', b)], wk)
            else:
                dcopy(dst, srcp, [('ps', b)], wk)

    def xkeys(k, g):
        return [('x', k, g, i) for i in range(4)]

    ADA_BANK = 7

    def ada_piece(l, nb):
        s = wload(d_ada_w[l].rearrange("(k p) n -> p k n", p=128)[:, :, nb * 512:(nb + 1) * 512], v3(8))
        sv = v3(8)(slots[s])
        for c in range(4):
            n = nb * 4 + c
            for k in range(8):
                mm(psb[ADA_BANK][:, n * 2:n * 2 + 2], sv[:, k, c * 128:(c + 1) * 128], cT[:, :, k],
                   k == 0, k == 7, [('slot', s)], [('ps', ADA_BANK)])
        wfree(s)

    def ada_finish(l):
        pv = psb[ADA_BANK][:, 0:96].rearrange("p (n c) -> p n c", c=2)
        for c in range(2):
            tt(mod[:, l, :, c], pv[:, :, c], vec('ada_b', l * 48, 48), ALU.add, [('ps', ADA_BANK)], [('mod', l, c)])
        for sub in range(2):
            nm = vec('norm_mix' if sub == 0 else 'norm_mlp', l * 8, 8)
            j = 1 if sub == 0 else 4
            for c in range(2):
                stt(AA[:, l, sub, c, :], mod[:, l, j * 8:(j + 1) * 8, c], 1.0, nm, ALU.add, ALU.mult,
                    [('mod', l, c)], [('AA', l, sub, c)])

    def modv(l, j, c, k):
        return mod[:, l, j * 8 + k:j * 8 + k + 1, c]

    ps_pool[0] = list(range(7))
    for nb in range(12):
        ada_piece(0, nb)
    ada_finish(0)
    P.barrier()

    def norm_mod(l, sub, g, dst_fn, dst_keys, sq, rstd, tmps, lnt):
        c = 1 if g < 4 else 0
        cols = slice(g * GS, (g + 1) * GS)
        xk = [k_ for k in range(8) for k_ in xkeys(k, g)]
        act(sq[:, :, :], xT[:, :, cols], AF.Square, xk, [('sq',)])
        b = psnext()
        for k in range(8):
            mm(psb[b][:, :], ones_m[:, 0, :], sq[:, k, :], k == 0, k == 7, [('sq',)], [('ps', b)])
        act(lnt[:, :], psb[b][:, :], AF.Ln, [('ps', b)], [('lnt',)], bias=epst[:, 0:1])
        act(rstd[:, :], lnt[:, :], AF.Exp, [('lnt',)], [('rstd',)], scale=-0.5)
        for k in range(8):
            tm = tmps[k % len(tmps)]
            tk = ('tmp', k % len(tmps))
            stt(tm[:, :], xT[:, k, cols], AA[:, l, sub, c, k:k + 1], rstd[:, :], ALU.mult, ALU.mult,
                xkeys(k, g) + [('rstd',)], [tk])
            act(dst_fn(k), tm[:, :], AF.Identity, [tk], [dst_keys(k)], bias=modv(l, 0 if sub == 0 else 3, c, k))

    def norm_all(l, sub):
        salloc_reset()
        sq = salloc([128, 8, GS], BF16)
        rstd = salloc([128, GS], F32)
        lnt = salloc([128, GS], F32)
        tmps = [salloc([128, GS], F32) for _ in range(3)]
        for g in range(NG):
            norm_mod(l, sub, g, lambda k, g=g: hT[:, k, g * GS:(g + 1) * GS], lambda k, g=g: ('h', k, g), sq, rstd, tmps, lnt)
        P.barrier()

    def mlp(l):
        salloc_reset()
        n_sq = salloc([128, 8, GS], BF16)
        n_rstd = salloc([128, GS], F32)
        n_lnt = salloc([128, GS], F32)
        n_tmps = [salloc([128, GS], F32) for _ in range(3)]

        def nrm(g):
            norm_mod(l, 1, g, lambda k, g=g: hT[:, k, g * GS:(g + 1) * GS], lambda k, g=g: ('h', k, g),
                     n_sq, n_rstd, n_tmps, n_lnt)
        nrm(0)
        nrm(1)
        aT = [salloc([128, 4, GS], BF16) for _ in range(2)]
        rl = [salloc([128, GS], F32) for _ in range(3)]
        ai = 0
        ri = 0
        ada_sched = [2, 1, 2, 1, 2, 1, 2, 1]
        ada_nb = 0
        def ldw(fg_):
            a_ = wload(d_w1[l].rearrange("(k p) n -> p k n", p=128)[:, :, fg_ * 512:(fg_ + 1) * 512], v3(8))
            b_ = wload(d_w2[l][fg_ * 512:(fg_ + 1) * 512, :].rearrange("(k p) n -> p k n", p=128), v3(4))
            return a_, b_
        nxt = ldw(0)
        for fg in range(8):
            s1, s2 = nxt
            if fg < 7:
                nxt = ldw(fg + 1)
            w1v = v3(8)(slots[s1]); w2v = v3(4)(slots[s2])
            for g in range(NG):
                if fg == 0 and g + 2 < NG:
                    nrm(g + 2)
                if l < 3 and g in (1, 3) and (g == 1 or ada_sched[fg] == 2):
                    ada_piece(l + 1, ada_nb); ada_nb += 1
                c = 1 if g < 4 else 0
                cols = slice(g * GS, (g + 1) * GS)
                a = aT[ai % 2]; ak = ai % 2; ai += 1
                for f in range(4):
                    b = psnext()
                    for k in range(8):
                        mm(psb[b][:, :], w1v[:, k, f * 128:(f + 1) * 128], hT[:, k, cols], k == 0, k == 7,
                           [('slot', s1), ('h', k, g)], [('ps', b)])
                    rt = rl[ri % 3]; rk = ('rl', ri % 3); ri += 1
                    act(rt[:, :], psb[b][:, :], AF.Relu, [('ps', b)], [rk])
                    act(a[:, f, :], rt[:, :], AF.Square, [rk], [('aT', ak, f)])
                for n in range(8):
                    b = psnext()
                    for f in range(4):
                        mm(psb[b][:, :], w2v[:, f, n * 128:(n + 1) * 128], a[:, f, :], f == 0, f == 3,
                           [('slot', s2), ('aT', ak, f)], [('ps', b)])
                    stt(xT[:, n, cols], psb[b][:, :], modv(l, 5, c, n), xT[:, n, cols], ALU.mult, ALU.add,
                        [('ps', b)] + xkeys(n, g), xkeys(n, g))
            wfree(s1); wfree(s2)
        if l < 3:
            assert ada_nb == 12, ada_nb
            ada_finish(l + 1)
        P.barrier()

    def rglru(l, j):
        import os
        KPOOL = int(os.environ.get('KPOOL', '1'))
        norm_all(l, 0)
        ps_pool[0] = [4, 5, 6, 7]
        salloc_reset()
        TS = 2048
        b1 = salloc([128, TS], F32)
        gy = salloc([128, TS], BF16)
        xc = salloc([128, TS], F32)
        u_shared = salloc([128, GS], BF16)
        pb = salloc([128, GS], F32)
        sets = [dict(r=salloc([128, GS], F32), a=salloc([128, GS], F32), b=salloc([128, GS], F32),
                     xb=salloc([128, GS], BF16), u=u_shared) for _ in range(2)]
        t_hbs = [salloc([128, GS], F32) for _ in range(2)]
        hbc = salloc([128, 1], F32)
        ui = [0]
        sg_ = wload(d_wa[j].rearrange("d n k j -> k (d n) j"), lambda t: t[:, 0:2048].rearrange("p (m j) -> p m j", j=128))
        sgx = None
        gav = slots[sg_][:, 0:2048].rearrange("p (m j) -> p m j", j=128)
        dstx = slots[sg_][:, 2048:4096].rearrange("p (m j) -> p m j", j=128)
        srcx = d_wx[j].rearrange("d n k j -> k (d n) j")
        P.op('pool', lambda e: e.dma_start(out=dstx, in_=srcx), (), [('slot', sg_)], dma=('slot', sg_))
        gxv = dstx
        for q in range(2):
            sx = wload(d_win[j].rearrange("(k p) n -> p k n", p=128)[:, :, q * 512:(q + 1) * 512], v3(8))
            sy = wload(d_win[j].rearrange("(k p) n -> p k n", p=128)[:, :, 1024 + q * 512:1024 + (q + 1) * 512], v3(8))
            so = wload(d_wout[j][q * 512:(q + 1) * 512, :].rearrange("(k p) n -> p k n", p=128), v3(4))
            wxv = v3(8)(slots[sx]); wyv = v3(8)(slots[sy]); wov = v3(4)(slots[so])
            for nn in range(4):
                n = q * 4 + nn
                cw = lambda tap: vec('conv_w', j * 32 + tap * 8 + n, 1)
                cb = vec('conv_b', j * 8 + n, 1)
                for pas in range(2):
                    if pas == 0:
                        groups = [0, 1, 2, 3]; segs = [(0, 2048, True)]; off = 0
                    else:
                        groups = [4]; segs = [(0, 256, False), (256, 512, False)]; off = 2048
                    nblk = len(groups)
                    for bi, g in enumerate(groups):
                        cols = slice(g * GS, (g + 1) * GS)
                        lc = slice(bi * GS, (bi + 1) * GS)
                        b = psnext()
                        for k in range(8):
                            mm(psb[b][:, :], wxv[:, k, nn * 128:(nn + 1) * 128], hT[:, k, cols], k == 0, k == 7,
                               [('slot', sx), ('h', k, g)], [('ps', b)])
                        act(b1[:, lc], psb[b][:, :], AF.Copy, [('ps', b)], [('b1', bi)])
                        b = psnext()
                        for k in range(8):
                            mm(psb[b][:, :], wyv[:, k, nn * 128:(nn + 1) * 128], hT[:, k, cols], k == 0, k == 7,
                               [('slot', sy), ('h', k, g)], [('ps', b)])
                        act(gy[:, lc], psb[b][:, :], AF.Gelu_apprx_tanh, [('ps', b)], [('gy', bi)])
                    allb1 = [('b1', bi) for bi in range(nblk)]
                    allxc = [('xc', bi) for bi in range(nblk)]
                    for (s0, s1, _) in segs:
                        ts(xc[:, s0:s1], b1[:, s0:s1], cw(1), cb, ALU.mult, ALU.add, allb1 + [('vecs',)], allxc)
                        stt(xc[:, s0 + 1:s1], b1[:, s0:s1 - 1], cw(0), xc[:, s0 + 1:s1], ALU.mult, ALU.add, allb1 + allxc, allxc)
                        stt(xc[:, s0:s1 - 1], b1[:, s0 + 1:s1], cw(2), xc[:, s0:s1 - 1], ALU.mult, ALU.add, allb1 + allxc, allxc)
                        stt(xc[:, s0:s1 - 2], b1[:, s0 + 2:s1], cw(3), xc[:, s0:s1 - 2], ALU.mult, ALU.add, allb1 + allxc, allxc)
                    units = []
                    for (s0, s1, _) in segs:
                        u0 = s0
                        while u0 < s1:
                            units.append((u0, min(u0 + GS, s1), s0, s1))
                            u0 += GS
                    ctxs = []
                    for d in range(2):
                        order = units if d == 0 else units[::-1]
                        for (u0, u1, s0, s1) in order:
                            sk = ui[0] % 2; ui[0] += 1
                            ctxs.append(dict(d=d, u0=u0, u1=u1, s0=s0, s1=s1, W=u1 - u0, bi=u0 // GS, lc=slice(u0, u1),
                                             S=sets[sk], sk=sk))

                    def st_copy_mm(c):
                        S = c['S']; sk = c['sk']; W = c['W']; d = c['d']
                        if KPOOL:
                            P.op('pool', (lambda o_, i_: (lambda e: e.tensor_copy(o_, i_)))(S['xb'][:, 0:W], xc[:, c['lc']]),
                                 allxc, [('t_xb', sk)])
                        else:
                            act(S['xb'][:, 0:W], xc[:, c['lc']], AF.Copy, allxc, [('t_xb', sk)])
                        bA = 2 * sk
                        mm(psb[bA][:, 0:W], gav[:, d * 8 + n, :], S['xb'][:, 0:W], True, True, [('slot', sg_), ('t_xb', sk)], [('ps', bA)])
                        bX = 2 * sk + 1
                        mm(psb[bX][:, 0:W], gxv[:, d * 8 + n, :], S['xb'][:, 0:W], True, True, [('slot', sg_), ('t_xb', sk)], [('ps', bX)])
                        c['bA'] = bA; c['bX'] = bX

                    def st_sig(c):
                        S = c['S']; sk = c['sk']; W = c['W']; d = c['d']
                        act(S['r'][:, 0:W], psb[c['bA']][:, 0:W], AF.Sigmoid, [('ps', c['bA'])], [('t_r', sk)], bias=vec('ba', j * 16 + d * 8 + n, 1))
                        act(S['b'][:, 0:W], psb[c['bX']][:, 0:W], AF.Sigmoid, [('ps', c['bX'])], [('t_b', sk)], bias=vec('bx', j * 16 + d * 8 + n, 1))

                    def st_exp(c):
                        S = c['S']; sk = c['sk']; W = c['W']; d = c['d']
                        col = j * 16 + d * 8 + n
                        kr = ('t_r', sk)
                        act(S['a'][:, 0:W], S['r'][:, 0:W], AF.Exp, [kr], [('t_a', sk)], scale=rgd[:, 0, col:col + 1])
                        act(S['r'][:, 0:W], S['r'][:, 0:W], AF.Exp, [kr], [kr], scale=rgd[:, 1, col:col + 1])
                        act(S['r'][:, 0:W], S['r'][:, 0:W], AF.Ln, [kr], [kr], scale=-1.0, bias=1.0)
                        act(S['r'][:, 0:W], S['r'][:, 0:W], AF.Exp, [kr], [kr], scale=0.5)
                        PE_ = 'pool' if KPOOL else 'dve'
                        P.op(PE_, (lambda o_, a_, b_: (lambda e: e.tensor_tensor(o_, a_, b_, ALU.mult)))(pb[:, 0:W], S['b'][:, 0:W], xc[:, c['lc']]),
                             [('t_b', sk)] + allxc, [('pb',)])
                        P.op(PE_, (lambda o_, a_, b_: (lambda e: e.tensor_tensor(o_, a_, b_, ALU.mult)))(S['b'][:, 0:W], pb[:, 0:W], S['r'][:, 0:W]),
                             [('pb',), kr], [('t_b', sk)])

                    def st_scan(c):
                        S = c['S']; sk = c['sk']; W = c['W']; d = c['d']
                        u0 = c['u0']; u1 = c['u1']; s0 = c['s0']; s1 = c['s1']; bi = c['bi']; lc = c['lc']
                        t_a = S['a']; t_b = S['b']
                        ka = ('t_a', sk); kb = ('t_b', sk)
                        t_hb = t_hbs[sk]; kh = ('t_hb', sk)
                        if d == 0:
                            if u0 == s0:
                                init = vec('state', j * 16 + 0 * 8 + n, 1) if pas == 0 else 0.0
                            else:
                                init = b1[:, u0 - 1:u0]
                            scan(b1[:, lc], t_a[:, 0:W], t_b[:, 0:W], init, [ka, kb] + allb1 + allxc, [('b1', bi)])
                            if pas == 1 and u1 == s1:
                                sq_i = (s0 // 256)
                                c_o = (sq_i * 2 + j) * 16 + 0 * 8 + n
                                dcopy(stst[:, c_o:c_o + 1], b1[:, s1 - 1:s1], [('b1', bi)], [('stst',)])
                        else:
                            if u1 == s1:
                                init = vec('state', j * 16 + 1 * 8 + n, 1) if pas == 0 else 0.0
                            else:
                                init = hbc[:, 0:1]
                            rv = lambda ap, W=W: ap[:, 0:W][:, ::-1]
                            scan(rv(t_hb), rv(t_a), rv(t_b), init, [ka, kb, ('hbc',)], [kh])
                            dcopy(hbc[:, 0:1], t_hb[:, 0:1], [kh], [('hbc',)])
                            if pas == 1 and u0 == s0:
                                sq_i = (s0 // 256)
                                c_o = (sq_i * 2 + j) * 16 + 1 * 8 + n
                                dcopy(stst[:, c_o:c_o + 1], t_hb[:, 0:1], [kh], [('stst',)])

                    def st_comb(c):
                        S = c['S']; sk = c['sk']; W = c['W']; d = c['d']
                        if d == 0:
                            return
                        u0 = c['u0']; bi = c['bi']; lc = c['lc']
                        t_hb = t_hbs[sk]; kh = ('t_hb', sk)
                        t_u = S['u']; ku = ('t_u',)
                        P.op('pool', (lambda o_, a_, b_: (lambda e: e.tensor_tensor(o_, a_, b_, ALU.add)))(pb[:, 0:W], b1[:, lc], t_hb[:, 0:W]),
                             [('b1', bi), kh], [('pb',)])
                        P.op('pool', (lambda o_, a_, b_: (lambda e: e.tensor_tensor(o_, a_, b_, ALU.mult)))(t_u[:, 0:W], pb[:, 0:W], gy[:, lc]),
                             [('pb',), ('gy', bi)], [ku])
                        gcol0 = off + u0
                        g = gcol0 // GS
                        cnd = 1 if g < 4 else 0
                        gc = slice(gcol0, gcol0 + W)
                        for m in range(8):
                            b = psnext()
                            mm(psb[b][:, 0:W], wov[:, nn, m * 128:(m + 1) * 128], t_u[:, 0:W], True, True,
                               [('slot', so), ku], [('ps', b)])
                            stt(xT[:, m, gc], psb[b][:, 0:W], modv(l, 2, cnd, m), xT[:, m, gc], ALU.mult, ALU.add,
                                [('ps', b)] + xkeys(m, g), xkeys(m, g))

                    pairs = [ctxs[i:i + 2] for i in range(0, len(ctxs), 2)]
                    for c in pairs[0]:
                        st_copy_mm(c)
                    for pi_, pr in enumerate(pairs):
                        for c in pr:
                            st_sig(c)
                        if pi_ + 1 < len(pairs):
                            for c in pairs[pi_ + 1]:
                                st_copy_mm(c)
                        for c in pr:
                            st_exp(c)
                        for c in pr:
                            st_scan(c)
                        for c in pr:
                            st_comb(c)
            wfree(sx); wfree(sy); wfree(so)
        wfree(sg_)
        ps_pool[0] = list(range(7))
        P.barrier()


    hbase = [0]

    def mla(l, j):
        import os
        SCALE = 192.0 ** -0.5
        salloc_reset()
        save = cur[0]
        cur[0] = 81920
        cqn = salloc([128, 4, T], BF16)
        ckvT = salloc([128, 2, 2816], BF16)
        kpT = salloc([64, 2816], BF16)
        assert cur[0] <= 81920 + 40960
        cur[0] = save
        hg = salloc([128, 8, GS], BF16)
        sq = salloc([128, 8, GS], BF16)
        rstd = salloc([128, GS], F32)
        lnt = salloc([128, GS], F32)
        tmps = [salloc([128, GS], F32) for _ in range(2)]
        cqf = salloc([128, 6, GS], F32)
        kpf = salloc([64, GS], F32)
        ost = salloc([128, 320], F32)
        s_q = wload(d_dqkv[j].rearrange("(k p) n -> p k n", p=128)[:, :, 0:512], v3(8))
        s_kv = wload(d_dqkv[j].rearrange("(k p) n -> p k n", p=128)[:, :, 512:832],
                     lambda t: t[:, 0:2560].rearrange("p (k n) -> p k n", k=8))
        wqv = v3(8)(slots[s_q])
        wkvv = slots[s_kv][:, 0:2560].rearrange("p (k n) -> p k n", k=8)
        ev = [0]

        def evac(dst, src, r, w, scale=None):
            ev[0] += 1
            if ev[0] % 2 == 0 or scale is not None:
                act(dst, src, AF.Copy, r, w, scale=scale)
            else:
                dcopy(dst, src, r, w)

        def rope(src, g, dst, rk, wk, Cg, Sg, t1, t2, ck, sk, k1, k2):
            cols = slice(g * GS, (g + 1) * GS)
            dma_sp(Cg[0:64, :], d_ropec[:, cols], (), [ck], 'ropeC')
            dma_sp(Sg[0:64, :], d_ropes[:, cols], (), [sk], 'ropeS')
            b = psnext()
            mm(psb[b][0:64, :], Rm, src, True, True, rk, [('ps', b)])
            tt(t1[0:64, :], src, Cg[0:64, :], ALU.mult, rk + [ck], [k1])
            tt(t2[0:64, :], psb[b][0:64, :], Sg[0:64, :], ALU.mult, [('ps', b), sk], [k2])
            tt(dst, t1[0:64, :], t2[0:64, :], ALU.add, [k1, k2], wk)

        for g in range(NG):
            cols = slice(g * GS, (g + 1) * GS)
            norm_mod(l, 0, g, lambda k: hg[:, k, :], lambda k: ('hg', k), sq, rstd, tmps, lnt)
            hk = [('hg', k) for k in range(8)]
            KP2 = int(os.environ.get('KP2', '9'))
            if KP2 < 1:
                continue
            for m in range(6 if KP2 != 1 else 4):
                b = psnext()
                for k in range(8):
                    lw = wqv[:, k, m * 128:(m + 1) * 128] if m < 4 else wkvv[:, k, (m - 4) * 128:(m - 3) * 128]
                    mm(psb[b][:, :], lw, hg[:, k, :], k == 0, k == 7, [('slot', s_q if m < 4 else s_kv), ('hg', k)], [('ps', b)])
                evac(cqf[:, m, :], psb[b][:, :], [('ps', b)], [('cqf', m)])
            if KP2 == 1:
                continue
            b = psnext()
            for k in range(8):
                mm(psb[b][0:64, :], wkvv[:, k, 256:320], hg[:, k, :], k == 0, k == 7, [('slot', s_kv), ('hg', k)], [('ps', b)])
            evac(kpf[:, :], psb[b][0:64, :], [('ps', b)], [('kpf',)])
            if KP2 < 2:
                continue
            act(sq[:, 0:6, :], cqf[:, :, :], AF.Square, [('cqf', m) for m in range(6)], [('sq',)])
            for (m0, m1, oi, nm, nbase) in ((0, 4, 1, 'normq', j * 4), (4, 6, 2, 'normkv', j * 2)):
                b = psnext()
                for m in range(m0, m1):
                    mm(psb[b][:, :], ones_m[:, oi, :], sq[:, m, :], m == m0, m == m1 - 1, [('sq',)], [('ps', b)])
                act(lnt[:, :], psb[b][:, :], AF.Ln, [('ps', b)], [('lnt',)], bias=epst[:, 0:1])
                act(rstd[:, :], lnt[:, :], AF.Exp, [('lnt',)], [('rstd',)], scale=-0.5)
                for m in range(m0, m1):
                    gam = vec(nm, nbase + (m - m0), 1)
                    if m < 4:
                        stt(cqn[:, m, cols], cqf[:, m, :], gam, rstd[:, :], ALU.mult, ALU.mult,
                            [('cqf', m), ('rstd',)], [('cqn', m, g)])
                    elif g < 4:
                        stt(ckvT[:, m - 4, cols], cqf[:, m, :], gam, rstd[:, :], ALU.mult, ALU.mult,
                            [('cqf', m), ('rstd',)], [('ckv', g * 4 + i) for i in range(4)])
                    else:
                        stt(cqf[:, m, :], cqf[:, m, :], gam, rstd[:, :], ALU.mult, ALU.mult,
                            [('cqf', m), ('rstd',)], [('cqf', m)])
                        act(ckvT[:, m - 4, 2304:2816], cqf[:, m, :], AF.Copy, [('cqf', m)], [('ckv', 18 + i) for i in range(4)])
            if KP2 < 3:
                continue
            if g < 4:
                rope(kpf[:, :], g, kpT[:, cols], [('kpf',)], [('kp', g * 4 + i) for i in range(4)],
                     lnt, rstd, tmps[0], tmps[1], ('lnt',), ('rstd',), ('tmp', 0), ('tmp', 1))
            else:
                act(kpT[:, 2304:2816], kpf[:, :], AF.Copy, [('kpf',)], [('kp', 18 + i) for i in range(4)])
                for t4 in range(4 if KP2 > 3 else 0):
                    b = psnext()
                    for m in range(2):
                        tr(psb[b][:, m * 128:(m + 1) * 128], cqf[:, 4 + m, t4 * 128:(t4 + 1) * 128], ident, [('cqf', 4 + m)], [('ps', b)])
                    tr(psb[b][:, 256:320], kpf[:, t4 * 128:(t4 + 1) * 128], ident[0:64, 0:64], [('kpf',)], [('ps', b)])
                    act(ost[:, 0:320], psb[b][:, 0:320], AF.Copy, [('ps', b)], [('ost',)])
                    sqi = t4 // 2
                    r0 = (t4 % 2) * 128
                    if KP2 != 5:
                        dma_sp(o_ckv[sqi, j, r0:r0 + 128, :], ost[:, 0:256], [('ost',)], [], 'out3')
                    if KP2 != 5 and KP2 != 6:
                        dma_sp(o_kpe[sqi, j, r0:r0 + 128, :], ost[:, 256:320], [('ost',)], [], 'out4')
        wfree(s_q); wfree(s_kv)
        P.barrier()

        import os
        if os.environ.get('KSKIP_P3'):
            return
        salloc_reset()
        qn = salloc([128, T], BF16)
        qp = salloc([64, T], BF16)
        kn = salloc([128, 2816], BF16)
        V = salloc([128, 22, 128], BF16)
        pTs = [salloc([128, GS], BF16) for _ in range(4)]
        oTs = [salloc([128, GS], BF16) for _ in range(3)]
        rz = salloc([128, GS], F32)
        qf = salloc([64, GS], F32)
        Cg = salloc([64, GS], F32); Sg = salloc([64, GS], F32)
        save3 = cur[0]
        cst1 = salloc([128, 2, 256], F32)
        cst2 = salloc([128, 2, 64], F32)
        cur[0] = save3
        t1 = salloc([64, GS], F32); t2 = salloc([64, GS], F32)
        dma_sp(cst1[:, :, :], d_cckv[j].rearrange("(t p) f -> p t f", p=128), (), [('cst1',)], 'in1')
        dma_sp(cst2[:, :, :], d_ckpe[j].rearrange("(t p) f -> p t f", p=128), (), [('cst2',)], 'in2')
        ps_pool[0] = [0, 1, 2, 3]
        b = psnext()
        for m in range(2):
            for t in range(2):
                tr(psb[b][:, (m * 2 + t) * 128:(m * 2 + t + 1) * 128], cst1[:, t, m * 128:(m + 1) * 128], ident, [('cst1',)], [('ps', b)])
        dcopy(ckvT[:, :, 2048:2304], psb[b][:, :].rearrange("p (m n) -> p m n", m=2), [('ps', b)], [('ckv', 16), ('ckv', 17)])
        b = psnext()
        for t in range(2):
            tr(psb[b][0:64, t * 128:(t + 1) * 128], cst2[:, t, :], ident, [('cst2',)], [('ps', b)])
        act(kpT[:, 2048:2304], psb[b][0:64, 0:256], AF.Copy, [('ps', b)], [('kp', 16), ('kp', 17)])

        s_uq = [wload(d_uq[j].rearrange("(k p) n -> p k n", p=128)[:, :, q * 768:(q + 1) * 768],
                      lambda t: t[:, 0:3072].rearrange("p (k n) -> p k n", k=4)) for q in range(2)]
        s_ukv = wload(d_ukv[j].rearrange("(k p) n -> p k n", p=128), v3(2))
        s_wo = [wload(d_wo[j].rearrange("(h p) n -> p h n", p=128)[:, :, q * 512:(q + 1) * 512], v3(8)) for q in range(2)]
        ukvv = v3(2)(slots[s_ukv])
        wov = [v3(8)(slots[s]) for s in s_wo]
        oz = [0]
        pi = [0]
        oi_ = [0]
        allckv = [('ckv', i) for i in range(22)]
        allkp = [('kp', i) for i in range(22)]
        for h in range(8):
            uqv = slots[s_uq[h // 4]][:, 0:3072].rearrange("p (k n) -> p k n", k=4)
            su = s_uq[h // 4]
            off = (h % 4) * 192
            for g in range(NG):
                cols = slice(g * GS, (g + 1) * GS)
                b = psnext()
                for k in range(4):
                    mm(psb[b][:, :], uqv[:, k, off:off + 128], cqn[:, k, cols], k == 0, k == 3, [('slot', su), ('cqn', k, g)], [('ps', b)])
                act(qn[:, cols], psb[b][:, :], AF.Copy, [('ps', b)], [('qn', g)], scale=SCALE)
                b = psnext()
                for k in range(4):
                    mm(psb[b][0:64, :], uqv[:, k, off + 128:off + 192], cqn[:, k, cols], k == 0, k == 3, [('slot', su), ('cqn', k, g)], [('ps', b)])
                if g < 4:
                    act(qf[:, :], psb[b][0:64, :], AF.Copy, [('ps', b)], [('qf',)], scale=SCALE)
                    rope(qf[:, :], g, qp[:, cols], [('qf',)], [('qp', g)], Cg, Sg, t1, t2, ('Cg',), ('Sg',), ('t1',), ('t2',))
                else:
                    act(qp[:, cols], psb[b][0:64, :], AF.Copy, [('ps', b)], [('qp', g)], scale=SCALE)
            for blk in range(6):
                c0 = blk * GS
                W = min(GS, 2816 - c0)
                b = psnext()
                for c in range(2):
                    mm(psb[b][:, 0:W], ukvv[:, c, h * 256:h * 256 + 128], ckvT[:, c, c0:c0 + W], c == 0, c == 1,
                       [('slot', s_ukv)] + [('ckv', blk * 4 + i) for i in range(W // 128)], [('ps', b)])
                evac(kn[:, c0:c0 + W], psb[b][:, 0:W], [('ps', b)], [('kn', blk)])
            for t0 in range(0, 22, 4):
                nt = min(4, 22 - t0)
                b = psnext()
                for q in range(nt):
                    tt_ = t0 + q
                    for c in range(2):
                        mm(psb[b][:, q * 128:(q + 1) * 128], ckvT[:, c, tt_ * 128:(tt_ + 1) * 128], ukvv[:, c, h * 256 + 128:h * 256 + 256],
                           c == 0, c == 1, [('slot', s_ukv), ('ckv', tt_)], [('ps', b)])
                evac(V[:, t0:t0 + nt, :], psb[b][:, 0:nt * 128].rearrange("p (q n) -> p q n", q=nt), [('ps', b)], [('V', t0 // 4)])
            jobs = [(qg * GS, GS, list(range(18)), qg, 1) for qg in range(4)]
            jobs += [(2048 + s_ * 256, 256, [18 + 2 * s_, 19 + 2 * s_], 4, 0) for s_ in range(2)]
            LAG = 3
            steps = []
            for ji, (q0, W, kts, g, cnd) in enumerate(jobs):
                for ki, kt in enumerate(kts):
                    steps.append((ji, ki, kt))
            jbank = {}
            pend = []
            fin_pe = []

            def emit_S(ji, ki, kt):
                (q0, W, kts, g, cnd) = jobs[ji]
                if ki == 0:
                    jbank[ji] = 4 + 2 * (oz[0] % 2); oz[0] += 1
                qc = slice(q0, q0 + W)
                kc = slice(kt * 128, (kt + 1) * 128)
                b = psnext()
                mm(psb[b][:, 0:W], kn[:, kc], qn[:, qc], True, False, [('kn', kt // 4), ('qn', g)], [('ps', b)])
                mm(psb[b][:, 0:W], kpT[:, kc], qp[:, qc], False, True, [('kp', kt), ('qp', g)], [('ps', b)])
                pT = pTs[pi[0] % 4]; pk = ('pT', pi[0] % 4); pi[0] += 1
                act(pT[:, 0:W], psb[b][:, 0:W], AF.Exp, [('ps', b)], [pk])
                return (pT, pk)

            def emit_PV(ji, ki, kt, pT, pk):
                (q0, W, kts, g, cnd) = jobs[ji]
                bO = jbank[ji]; bZ = bO + 1
                last = ki == len(kts) - 1
                mm(psb[bO][:, 0:W], V[:, kt, :], pT[:, 0:W], ki == 0, last, [('V', kt // 4), pk], [('ps', bO)])
                mm(psb[bZ][:, 0:W], ones_1[:, :], pT[:, 0:W], ki == 0, last, [pk], [('ps', bZ)])
                if last:
                    qc = slice(q0, q0 + W)
                    recip(rz[:, 0:W], psb[bZ][:, 0:W], [('ps', bZ)], [('rz',)])
                    oT = oTs[oi_[0] % 3]; ok = ('oT', oi_[0] % 3); oi_[0] += 1
                    tt(oT[:, 0:W], psb[bO][:, 0:W], rz[:, 0:W], ALU.mult, [('ps', bO), ('rz',)], [ok])

                    def fin(W=W, qc=qc, g=g, cnd=cnd, oT=oT, ok=ok):
                        for m in range(8):
                            b = psnext()
                            mm(psb[b][:, 0:W], wov[m // 4][:, h, (m % 4) * 128:(m % 4 + 1) * 128], oT[:, 0:W], True, True,
                               [('slot', s_wo[m // 4]), ok], [('ps', b)])
                            xk = xkeys(m, g)
                            stt(xT[:, m, qc], psb[b][:, 0:W], modv(l, 2, cnd, m), xT[:, m, qc], ALU.mult, ALU.add,
                                [('ps', b)] + xk, xk)
                    fin_pe.append([LAG + 1, fin])

            def tick_fin(force=False):
                for it in list(fin_pe):
                    it[0] -= 1
                    if it[0] <= 0 or force:
                        it[1]()
                        fin_pe.remove(it)

            for (ji, ki, kt) in steps:
                pT, pk = emit_S(ji, ki, kt)
                pend.append((ji, ki, kt, pT, pk))
                tick_fin()
                if len(pend) > LAG:
                    emit_PV(*pend.pop(0))
            while pend:
                tick_fin()
                emit_PV(*pend.pop(0))
            tick_fin(force=True)
        for s_ in s_uq + [s_ukv] + s_wo:
            wfree(s_)
        ps_pool[0] = list(range(7))
        P.barrier()

    def final_out():
        salloc_reset()
        sq = salloc([128, 8, GS], BF16)
        rstd = salloc([128, GS], F32)
        lnt = salloc([128, GS], F32)
        yT = salloc([128, 8, GS], F32)
        ostg = [salloc([128, 1024], F32) for _ in range(2)]
        oi = 0
        for g in range(NG):
            cols = slice(g * GS, (g + 1) * GS)
            xk = [k_ for k in range(8) for k_ in xkeys(k, g)]
            act(sq[:, :, :], xT[:, :, cols], AF.Square, xk, [('sq',)])
            b = psnext()
            for k in range(8):
                mm(psb[b][:, :], ones_m[:, 0, :], sq[:, k, :], k == 0, k == 7, [('sq',)], [('ps', b)])
            act(lnt[:, :], psb[b][:, :], AF.Ln, [('ps', b)], [('lnt',)], bias=epst[:, 0:1])
            act(rstd[:, :], lnt[:, :], AF.Exp, [('lnt',)], [('rstd',)], scale=-0.5)
            for k in range(8):
                stt(yT[:, k, :], xT[:, k, cols], vec('final', k, 1), rstd[:, :], ALU.mult, ALU.mult,
                    xkeys(k, g) + [('rstd',)], [('yT', k)])
            for t4 in range(4):
                og = ostg[oi % 2]; ok = ('ostg', oi % 2)
                for half in range(2):
                    b = psnext()
                    for q in range(4):
                        k = half * 4 + q
                        tr(psb[b][:, q * 128:(q + 1) * 128], yT[:, k, t4 * 128:(t4 + 1) * 128], ident, [('yT', k)], [('ps', b)])
                    if half == 0:
                        act(og[:, 0:512], psb[b][:, :], AF.Copy, [('ps', b)], [(ok, 0)])
                    else:
                        dcopy(og[:, 512:1024], psb[b][:, :], [('ps', b)], [(ok, 1)])
                row0 = g * GS + t4 * 128
                dst = o_ys[row0:row0 + 128, :] if g < 4 else o_yp[row0 - 2048:row0 - 2048 + 128, :]
                dma_sp(dst, og[:, :], [(ok, 0), (ok, 1)], [], 'out%d' % (oi % 2))
                oi += 1
        b = psnext()
        tr(psb[b][0:64, 0:128], stst[:, :], ident, [('stst',)], [('ps', b)])
        so_ = salloc([64, 128], F32)
        dcopy(so_[:, :], psb[b][0:64, 0:128], [('ps', b)], [('so_',)])
        dma_sp(o_st, so_[:, :], [('so_',)], [], 'out2')

    import os
    steps = os.environ.get('KSTEPS', 'full')
    if steps == 'full':
        for l in range(4):
            j = l // 2
            if l % 2 == 0:
                rglru(l, j)
            else:
                mla(l, j)
            mlp(l)
    else:
        for st_ in steps.split(','):
            if st_ == 'mla':
                mla(0, 0)
            elif st_ == 'rg':
                rglru(0, 0)
            elif st_ == 'mlp':
                mlp(0)
    final_out()

    P.emit(['out0', 'out1', 'out2', 'out3', 'out4'])
    return nc


_CACHE = {}


def rope_tables():
    n = 2048
    row = np.repeat(np.arange(n // 64, dtype=np.float32), 64)
    col = np.tile(np.arange(64, dtype=np.float32), n // 64)
    half = 32
    inv = (10000.0 ** (-np.arange(0, half, 2, dtype=np.float32) / half)).astype(np.float32)
    ang = np.concatenate([row[:, None] * inv, col[:, None] * inv], axis=-1)
    c = np.cos(ang).astype(np.float32); s = np.sin(ang).astype(np.float32)
    C = np.repeat(c, 2, axis=1).T
    S = np.repeat(s, 2, axis=1).T
    return np.ascontiguousarray(C), np.ascontiguousarray(S)


def kernel(**inp):
    f32 = lambda a: np.ascontiguousarray(np.asarray(a, dtype=np.float32))
    if 'nc' not in _CACHE:
        _CACHE['nc'] = build_program()
    nc = _CACHE['nc']
    C, S = rope_tables()
    cstm = np.zeros((128, 192), np.float32)
    cstm[:, 0:128] = np.eye(128, dtype=np.float32)
    for i in range(32):
        cstm[2 * i + 1, 128 + 2 * i] = -1.0
        cstm[2 * i, 128 + 2 * i + 1] = 1.0
    shared = {k: f32(inp[k]) for k in ['ada_w', 'mlp_w1', 'mlp_w2', 'rg_w_in', 'rg_wa', 'rg_wx', 'rg_w_out',
                                      'mla_w_dqkv', 'mla_w_uq', 'mla_w_ukv', 'mla_w_o']}
    in_maps = []
    for c in range(NCORE):
        parts = {
            'ada_b': fm(inp['ada_b']), 'norm_mix': fm(inp['norm_mix']), 'norm_mlp': fm(inp['norm_mlp']),
            'final': fm(inp['final_norm']), 'conv_w': fm(inp['rg_conv_w']), 'conv_b': fm(inp['rg_conv_b']),
            'ba': fm(inp['rg_ba']), 'bx': fm(inp['rg_bx']), 'lam': fm(inp['rg_lambda']),
            'normq': fm(inp['mla_norm_q']), 'normkv': fm(inp['mla_norm_kv']),
            'state': fm(np.asarray(inp['state_rglru'])[c]),
            'cond': fm(np.stack([np.asarray(inp['c_ctx']), np.asarray(inp['c'])[c]], 0)),
        }
        vecs = np.concatenate([parts[n] for n, _ in VEC_SPEC], axis=1)
        assert vecs.shape == (128, NV), vecs.shape
        m = dict(shared)
        m.update({
            'xs': f32(np.asarray(inp['x_sample'])[c]),
            'xp': f32(np.asarray(inp['x_prompt'])[2 * c:2 * c + 2].reshape(512, 1024)),
            'cckv': f32(np.asarray(inp['cache_ckv'])[c]), 'ckpe': f32(np.asarray(inp['cache_kpe'])[c]),
            'vecs': f32(vecs), 'cst': cstm, 'ropec': C, 'ropes': S,
        })
        in_maps.append(m)
    res = run_bass_kernel_spmd(nc, in_maps, core_ids=list(range(NCORE)))
    R = res.results
    y_prompt = np.concatenate([R[c]['yp'].reshape(2, 256, 1024) for c in range(NCORE)], 0)
    y_sample = np.stack([R[c]['ys'] for c in range(NCORE)], 0)
    nstate = np.concatenate([R[c]['nstate'].reshape(2, 2, 2, 1024) for c in range(NCORE)], 0)
    nckv = np.concatenate([R[c]['nckv'] for c in range(NCORE)], 0)
    nkpe = np.concatenate([R[c]['nkpe'] for c in range(NCORE)], 0)
    return (y_prompt.astype(np.float32), y_sample.astype(np.float32), nstate.astype(np.float32),
            nckv.astype(np.float32), nkpe.astype(np.float32))
```
